# Optimizing a Trainium2 kernel written in Bass

```python
import jax, jax.numpy as jnp
from jax import lax
import numpy as np

D_MODEL = 1024
BATCH = 32
SEQ = 2048
DEPTH = 1
DEC_BATCH = 32
DEC_SEQ = 32
PAST_LEN = 4096

CHUNK = 64
D_MIX = D_MODEL
D_LRU = D_MIX // 2
D_CONV = D_MIX - D_LRU
LRU_HEADS = 8
LRU_HEAD_DIM = D_LRU // LRU_HEADS
LRU_CONV_W = 4
LRU_C = 8.0
CM_KERNEL = 31
D_FF = ((8 * D_MODEL // 3 + 127) // 128) * 128
N_MOD = 9
EPS = 1e-6
ADA_SCALE = 0.02

kernel_name = "hybrid_rglru_conformer_stream_step"


def rms_norm(x, g):
    xf = x.astype(jnp.float32)
    y = xf * lax.rsqrt(jnp.mean(xf * xf, axis=-1, keepdims=True) + EPS)
    return (y * g.astype(jnp.float32)).astype(x.dtype)


def layer_norm(x, g, b):
    xf = x.astype(jnp.float32)
    mu = jnp.mean(xf, axis=-1, keepdims=True)
    xc = xf - mu
    y = xc * lax.rsqrt(jnp.mean(xc * xc, axis=-1, keepdims=True) + EPS)
    return (y * g.astype(jnp.float32) + b.astype(jnp.float32)).astype(x.dtype)


def modulate(x, g, shift, scale):
    return rms_norm(x, g) * (1 + scale[:, None, :]) + shift[:, None, :]


def swiglu(h, wg, wu, wd):
    return (jax.nn.silu(h @ wg) * (h @ wu)) @ wd


def causal_dwconv(x_full, w, b):
    C = x_full.shape[-1]
    out = lax.conv_general_dilated(
        x_full, w[:, None, :].astype(x_full.dtype), window_strides=(1,), padding="VALID",
        dimension_numbers=("NWC", "WIO", "NWC"), feature_group_count=C)
    return out + b


def block_diag(y, w):
    B, T, _ = y.shape
    yh = y.reshape(B, T, LRU_HEADS, LRU_HEAD_DIM)
    return jnp.einsum("bthi,hij->bthj", yh, w).reshape(B, T, D_LRU)


def rg_lru(y, z_gate, h0, wa, ba, wx, bx, lam):
    yf = y.astype(jnp.float32)
    r = jax.nn.sigmoid(block_diag(y, wa) + ba).astype(jnp.float32)
    i = jax.nn.sigmoid(block_diag(y, wx) + bx).astype(jnp.float32)
    log_a = -LRU_C * r * jax.nn.softplus(-lam.astype(jnp.float32))
    a = jnp.exp(log_a)
    b = jnp.sqrt(-jnp.expm1(2.0 * log_a)) * (i * yf)

    def combine(left, right):
        a1, b1 = left
        a2, b2 = right
        return a1 * a2, a2 * b1 + b2

    a_cum, b_cum = lax.associative_scan(combine, (a, b), axis=1)
    h = a_cum * h0.astype(jnp.float32)[:, None, :] + b_cum
    out = h.astype(y.dtype) * jax.nn.gelu(z_gate)
    return out, h[:, -1].astype(h0.dtype)


def stream_layer(x, c, lru_h0, lru_buf, cm_buf, w_ada, b_ada,
                 ffn1_g, ffn1_wg, ffn1_wu, ffn1_wd, mix_g, w_in,
                 lru_conv_w, lru_conv_b, lru_wa, lru_ba, lru_wx, lru_bx, lru_lambda,
                 cm_dw_w, cm_dw_b, cm_ln_g, cm_ln_b, out_g_lru, out_g_conv, w_out,
                 ffn2_g, ffn2_wg, ffn2_wu, ffn2_wd):
    B = c.shape[0]
    mod = (jax.nn.silu(c) @ w_ada + b_ada).reshape(B, N_MOD, D_MODEL)
    sh1, sc1, gt1, sh2, sc2, gt2, sh3, sc3, gt3 = [mod[:, k] for k in range(N_MOD)]

    h = modulate(x, ffn1_g, sh1, sc1)
    x = x + 0.5 * gt1[:, None, :] * swiglu(h, ffn1_wg, ffn1_wu, ffn1_wd)

    h = modulate(x, mix_g, sh2, sc2)
    z = h @ w_in
    z_lx, z_lg, z_cv, z_cg = jnp.split(z, [D_LRU, 2 * D_LRU, 2 * D_LRU + D_CONV], axis=-1)

    lru_full = jnp.concatenate([lru_buf.astype(z_lx.dtype), z_lx], axis=1)
    y_l = causal_dwconv(lru_full, lru_conv_w, lru_conv_b)
    lru_out, h_last = rg_lru(y_l, z_lg, lru_h0, lru_wa, lru_ba, lru_wx, lru_bx, lru_lambda)

    u = z_cv * jax.nn.sigmoid(z_cg)
    cm_full = jnp.concatenate([cm_buf.astype(u.dtype), u], axis=1)
    v = causal_dwconv(cm_full, cm_dw_w, cm_dw_b)
    v = jax.nn.silu(layer_norm(v, cm_ln_g, cm_ln_b))

    merged = jnp.concatenate([rms_norm(lru_out, out_g_lru), rms_norm(v, out_g_conv)], axis=-1) @ w_out
    x = x + gt2[:, None, :] * merged

    h = modulate(x, ffn2_g, sh3, sc3)
    x = x + 0.5 * gt3[:, None, :] * swiglu(h, ffn2_wg, ffn2_wu, ffn2_wd)
    return x, h_last, lru_full[:, -(LRU_CONV_W - 1):], cm_full[:, -(CM_KERNEL - 1):]


def setup_inputs(seed: int = 0) -> dict:
    key = jax.random.key(seed)
    ks = jax.random.split(key, 40)

    def nrm(k, shape, scale):
        return jax.random.normal(k, shape, jnp.float32) * scale

    def gain(k, shape):
        return 1.0 + 0.1 * jax.random.normal(k, shape, jnp.float32)

    u = jax.random.uniform(ks[21], (DEPTH, D_LRU), jnp.float32, minval=0.9, maxval=0.999)
    a_base = u ** (1.0 / LRU_C)
    lru_lambda = jnp.log(a_base) - jnp.log1p(-a_base)

    return {
        "x_prompt": nrm(ks[0], (BATCH, SEQ, D_MODEL), 1.0),
        "x_sample": nrm(ks[1], (DEC_BATCH, DEC_SEQ, D_MODEL), 1.0),
        "state_lru_h": nrm(ks[2], (DEPTH, DEC_BATCH, D_LRU), 0.5),
        "state_lru_conv": nrm(ks[3], (DEPTH, DEC_BATCH, LRU_CONV_W - 1, D_LRU), 1.0),
        "state_cm_conv": nrm(ks[4], (DEPTH, DEC_BATCH, CM_KERNEL - 1, D_CONV), 1.0),
        "c_prompt": nrm(ks[5], (BATCH, D_MODEL), 1.0),
        "c_sample": nrm(ks[6], (DEC_BATCH, D_MODEL), 1.0),
        "w_ada": nrm(ks[7], (DEPTH, D_MODEL, N_MOD * D_MODEL), ADA_SCALE),
        "b_ada": nrm(ks[8], (DEPTH, N_MOD * D_MODEL), 0.01),
        "ffn1_g": gain(ks[9], (DEPTH, D_MODEL)),
        "ffn1_wg": nrm(ks[10], (DEPTH, D_MODEL, D_FF), D_MODEL ** -0.5),
        "ffn1_wu": nrm(ks[11], (DEPTH, D_MODEL, D_FF), D_MODEL ** -0.5),
        "ffn1_wd": nrm(ks[12], (DEPTH, D_FF, D_MODEL), D_FF ** -0.5),
        "mix_g": gain(ks[13], (DEPTH, D_MODEL)),
        "w_in": nrm(ks[14], (DEPTH, D_MODEL, 2 * D_LRU + 2 * D_CONV), D_MODEL ** -0.5),
        "lru_conv_w": nrm(ks[15], (DEPTH, LRU_CONV_W, D_LRU), LRU_CONV_W ** -0.5),
        "lru_conv_b": nrm(ks[16], (DEPTH, D_LRU), 0.01),
        "lru_wa": nrm(ks[17], (DEPTH, LRU_HEADS, LRU_HEAD_DIM, LRU_HEAD_DIM), LRU_HEAD_DIM ** -0.5),
        "lru_ba": nrm(ks[18], (DEPTH, D_LRU), 0.01),
        "lru_wx": nrm(ks[19], (DEPTH, LRU_HEADS, LRU_HEAD_DIM, LRU_HEAD_DIM), LRU_HEAD_DIM ** -0.5),
        "lru_bx": nrm(ks[20], (DEPTH, D_LRU), 0.01),
        "lru_lambda": lru_lambda,
        "cm_dw_w": nrm(ks[22], (DEPTH, CM_KERNEL, D_CONV), CM_KERNEL ** -0.5),
        "cm_dw_b": nrm(ks[23], (DEPTH, D_CONV), 0.01),
        "cm_ln_g": gain(ks[24], (DEPTH, D_CONV)),
        "cm_ln_b": nrm(ks[25], (DEPTH, D_CONV), 0.01),
        "out_g_lru": gain(ks[26], (DEPTH, D_LRU)),
        "out_g_conv": gain(ks[27], (DEPTH, D_CONV)),
        "w_out": nrm(ks[28], (DEPTH, D_MIX, D_MODEL), D_MIX ** -0.5),
        "ffn2_g": gain(ks[29], (DEPTH, D_MODEL)),
        "ffn2_wg": nrm(ks[30], (DEPTH, D_MODEL, D_FF), D_MODEL ** -0.5),
        "ffn2_wu": nrm(ks[31], (DEPTH, D_MODEL, D_FF), D_MODEL ** -0.5),
        "ffn2_wd": nrm(ks[32], (DEPTH, D_FF, D_MODEL), D_FF ** -0.5),
        "final_g": gain(ks[33], (D_MODEL,)),
    }


def reference(x_prompt, x_sample, state_lru_h, state_lru_conv, state_cm_conv, c_prompt, c_sample,
              w_ada, b_ada, ffn1_g, ffn1_wg, ffn1_wu, ffn1_wd, mix_g, w_in,
              lru_conv_w, lru_conv_b, lru_wa, lru_ba, lru_wx, lru_bx, lru_lambda,
              cm_dw_w, cm_dw_b, cm_ln_g, cm_ln_b, out_g_lru, out_g_conv, w_out,
              ffn2_g, ffn2_wg, ffn2_wu, ffn2_wd, final_g):
    xp, xs = x_prompt, x_sample
    Bp = x_prompt.shape[0]
    hp_list, lcp_list, ccp_list = [], [], []
    hs_list, lcs_list, ccs_list = [], [], []
    for l in range(DEPTH):
        w = (w_ada[l], b_ada[l], ffn1_g[l], ffn1_wg[l], ffn1_wu[l], ffn1_wd[l], mix_g[l], w_in[l],
             lru_conv_w[l], lru_conv_b[l], lru_wa[l], lru_ba[l], lru_wx[l], lru_bx[l], lru_lambda[l],
             cm_dw_w[l], cm_dw_b[l], cm_ln_g[l], cm_ln_b[l], out_g_lru[l], out_g_conv[l], w_out[l],
             ffn2_g[l], ffn2_wg[l], ffn2_wu[l], ffn2_wd[l])
        h0_p = jnp.zeros((Bp, D_LRU), x_prompt.dtype)
        lbuf_p = jnp.zeros((Bp, LRU_CONV_W - 1, D_LRU), x_prompt.dtype)
        cbuf_p = jnp.zeros((Bp, CM_KERNEL - 1, D_CONV), x_prompt.dtype)
        xp, hp, lcp, ccp = stream_layer(xp, c_prompt, h0_p, lbuf_p, cbuf_p, *w)
        xs, hs, lcs, ccs = stream_layer(xs, c_sample, state_lru_h[l], state_lru_conv[l], state_cm_conv[l], *w)
        hp_list.append(hp); lcp_list.append(lcp); ccp_list.append(ccp)
        hs_list.append(hs); lcs_list.append(lcs); ccs_list.append(ccs)
    y_prompt = rms_norm(xp, final_g)
    y_sample = rms_norm(xs, final_g)
    new_lru_h_p = jnp.stack(hp_list, axis=0)
    new_lru_conv_p = jnp.stack(lcp_list, axis=0)
    new_cm_conv_p = jnp.stack(ccp_list, axis=0)
    new_lru_h_s = jnp.stack(hs_list, axis=0)
    new_lru_conv_s = jnp.stack(lcs_list, axis=0)
    new_cm_conv_s = jnp.stack(ccs_list, axis=0)
    return (y_prompt, y_sample, new_lru_h_p, new_lru_conv_p, new_cm_conv_p, new_lru_h_s, new_lru_conv_s, new_cm_conv_s)
```

```python
import contextlib
import numpy as np
import concourse.bass as bass
import concourse.mybir as mybir
from concourse.bass_utils import run_bass_kernel_spmd

F32 = mybir.dt.float32
BF16 = mybir.dt.bfloat16
AF = mybir.ActivationFunctionType
ALU = mybir.AluOpType

NCORES = 8
D = 1024
DFF = 2816
NJ = DFF // 128
DL = 512
SEQ = 2048
DSEQ = 32
PSEQ_PER_CORE = 4
SSEQ_PER_CORE = 4
NPT = PSEQ_PER_CORE * SEQ
NST = SSEQ_PER_CORE * DSEQ
NSEQ = 8
EPS = 1e-6
NT = 512


class _FakeIns:
    def then_inc(self, *a, **k):
        return self


class _FakeEng:
    def __init__(self):
        self.rec = None

    def __getattr__(self, name):
        def f(*a, **kw):
            self.rec = (name, a, kw)
            return _FakeIns()
        return f


def _fsz(ap):
    sh = ap.shape
    n = 1
    for d in sh[1:]:
        n *= int(d)
    return n


def _cost(eng, rec):
    name, a, kw = rec
    out = kw.get('out', a[0] if a else None)
    try:
        n = _fsz(out) if out is not None else 1
    except Exception:
        n = 1
    if name == 'dma_start':
        try:
            nbytes = out.size() * (2 if out.dtype == BF16 else 4)
        except Exception:
            nbytes = 1 << 16
        return 2500.0 + nbytes / 150.0
    if eng == 'pe':
        if name == 'transpose':
            return 160.0
        rhs = kw.get('rhs')
        nn = _fsz(rhs)
        mul = 4.0 if rhs.dtype == F32 else 1.0
        return max(nn, 64) / 2.4 * mul + 4.0
    if eng == 'act':
        return 230.0 + 0.96 * n + (190.0 if kw.get('accum_out') is not None else 0.0)
    if eng == 'dve':
        if name == 'reciprocal':
            return 100.0 + 6.1 * n
        if name == 'tensor_tensor_scan':
            return 70.0 + 2.0 * n
        if name in ('tensor_tensor', 'scalar_tensor_tensor'):
            return 190.0 + 1.0 * n
        if name == 'memset':
            return 100.0 + 0.5 * n
        return 200.0 + 0.85 * n
    if eng == 'pool':
        if name == 'tensor_tensor':
            return 250.0 + 2.05 * n
        if name == 'memset':
            return 150.0 + 1.0 * n
        return 280.0 + 3.0 * n
    return 30.0


def _tbl(rec):
    name, a, kw = rec
    if name != 'activation':
        return None
    f = kw.get('func')
    if f in (AF.Sqrt,):
        return 'sqrt'
    if f in (AF.Tanh, AF.Exp):
        return 'tanhexp'
    if f in (AF.Silu,):
        return 'silu'
    if f in (AF.Ln,):
        return 'ln'
    return None


class Sched:
    ENGS = ['pe', 'act', 'dve', 'pool', 'sp']
    WINDOW = 700

    def __init__(self):
        self.ops = []
        self.lastw = {}
        self.readers = {}
        self.region = 0
        self.reorder = {0: True}

    def barrier(self, reorder=True):
        self.region += 1
        self.reorder[self.region] = reorder

    def add(self, eng, fn, reads=(), writes=(), slot=None):
        idx = len(self.ops)
        deps = {}
        for r in reads:
            w = self.lastw.get(r)
            if w is not None:
                deps[w] = True
        for r in writes:
            w = self.lastw.get(r)
            if w is not None:
                deps.setdefault(w, False)
            for rd in self.readers.get(r, ()):
                deps.setdefault(rd, False)
        keep = []
        alld = []
        for d, raw in deps.items():
            o = self.ops[d]
            if o['region'] != self.region:
                continue
            alld.append(d)
            if o['slot'] is None and o['eng'] == eng and slot is None and eng == 'pe':
                continue
            keep.append(d)
        fk = _FakeEng()
        fn(fk)
        rec = fk.rec
        op = dict(eng=eng, fn=fn, deps=keep, alld=alld, slot=slot, val=None, used=False, region=self.region,
                  cost=_cost(eng, rec), tbl=_tbl(rec), idx=idx)
        for d in keep:
            self.ops[d]['used'] = True
        self.ops.append(op)
        for r in reads:
            self.readers.setdefault(r, []).append(idx)
        for r in writes:
            self.lastw[r] = idx
            self.readers[r] = []
        return idx

    def schedule(self):
        ops = self.ops
        nreg = self.region + 1
        byreg = [[] for _ in range(nreg)]
        for o in ops:
            byreg[o['region']].append(o['idx'])
        order = {e: [] for e in self.ENGS}
        regend = []
        tfree = {e: 0.0 for e in self.ENGS}
        fin = {}
        lasttbl = [None]
        for r in range(nreg):
            ids = byreg[r]
            t0 = max(tfree.values()) if r > 0 else 0.0
            for e in self.ENGS:
                tfree[e] = max(tfree[e], t0)
            ndep = {i: len(ops[i]['alld']) for i in ids}
            users = {i: [] for i in ids}
            for i in ids:
                for d in ops[i]['alld']:
                    users[d].append(i)
            ready = {e: [] for e in self.ENGS}
            pos = 0
            done = set()
            for i in ids:
                if ndep[i] == 0:
                    ready[ops[i]['eng']].append(i)
            nsched = 0
            ntot = len(ids)
            strict = ['sp'] if self.reorder.get(r, True) else list(self.ENGS)
            elist = {e: [i for i in ids if ops[i]['eng'] == e] for e in strict}
            eptr = {e: 0 for e in strict}
            idpos = {i: k for k, i in enumerate(ids)}
            while nsched < ntot:
                while pos < ntot and ids[pos] in done:
                    pos += 1
                lim = ids[min(pos + self.WINDOW, ntot - 1)] if pos < ntot else 0
                best = None
                for e in self.ENGS:
                    lst = ready[e]
                    if not lst:
                        continue
                    cands = lst
                    if e in eptr:
                        el = elist[e]
                        while eptr[e] < len(el) and el[eptr[e]] in done:
                            eptr[e] += 1
                        if eptr[e] >= len(el) or el[eptr[e]] not in lst:
                            continue
                        cands = [el[eptr[e]]]
                    for i in cands:
                        if i > lim:
                            continue
                        o = ops[i]
                        st = tfree[e]
                        for d in o['alld']:
                            if fin[d] > st:
                                st = fin[d]
                        pen = 0.0
                        if e == 'act' and o['tbl'] is not None and lasttbl[0] is not None and o['tbl'] != lasttbl[0]:
                            pen = 1300.0
                        key = (st + pen, i)
                        if best is None or key < best[0]:
                            best = (key, i, st, pen)
                if best is None:
                    allr = [i for e in self.ENGS for i in ready[e]]
                    i = min(allr)
                    o = ops[i]
                    st = tfree[o['eng']]
                    for d in o['alld']:
                        st = max(st, fin[d])
                    best = ((st, i), i, st, 0.0)
                _, i, st, pen = best
                o = ops[i]
                e = o['eng']
                ready[e].remove(i)
                if o['slot'] is not None:
                    tfree[e] = st + 60.0
                    fin[i] = st + o['cost']
                else:
                    if e == 'act' and o['tbl'] is not None:
                        lasttbl[0] = o['tbl']
                    tfree[e] = st + pen + o['cost']
                    fin[i] = tfree[e] + (200.0 if e == 'pe' else 60.0)
                order[e].append(i)
                done.add(i)
                nsched += 1
                for u in users[i]:
                    ndep[u] -= 1
                    if ndep[u] == 0:
                        ready[ops[u]['eng']].append(u)
            regend.append({e: len(order[e]) for e in self.ENGS})
        self.order = order
        self.regend = regend
        self.makespan = max(tfree.values())

    def emit(self, nc, stack):
        self.schedule()
        ops = self.ops
        engs = self.ENGS
        order = self.order
        slots = sorted({o['slot'] for o in ops if o['slot'] is not None})
        esem = {e: stack.enter_context(nc.semaphore("s_" + e)) for e in engs}
        ssem = {s: stack.enter_context(nc.semaphore("d_" + s)) for s in slots}
        nreg = len(self.regend)
        for r in range(nreg):
            for e in engs:
                lo = self.regend[r - 1][e] if r > 0 else 0
                hi = self.regend[r][e]
                for k in range(hi - 1, lo - 1, -1):
                    o = ops[order[e][k]]
                    if o['slot'] is None:
                        o['used'] = True
                        break
        cnt = {e: 0 for e in engs}
        scnt = {s: 0 for s in slots}
        ecnt_at = [dict() for _ in range(nreg)]
        scnt_at = [dict() for _ in range(nreg)]
        for e in engs:
            r = 0
            for k, i in enumerate(order[e]):
                o = ops[i]
                if o['slot'] is None:
                    if o['used']:
                        cnt[e] += 1
                        o['val'] = cnt[e]
        for e in engs:
            for i in order[e]:
                o = ops[i]
                if o['slot'] is not None:
                    scnt[o['slot']] += 16
                    o['val'] = scnt[o['slot']]
        for r in range(nreg):
            for e in engs:
                hi = self.regend[r][e]
                c = 0
                sc = {}
                for i in order[e][:hi]:
                    o = ops[i]
                    if o['slot'] is None:
                        if o['used']:
                            c = o['val']
                    else:
                        sc[o['slot']] = o['val']
                ecnt_at[r][e] = c
                for s_, v in sc.items():
                    scnt_at[r][s_] = max(scnt_at[r].get(s_, 0), v)
        block = stack.enter_context(nc.Block())

        def run(engname, e):
            known = {}

            def wait(key, sem, v):
                if v <= 0 or known.get(key, 0) >= v:
                    return
                e.wait_ge(sem, v)
                known[key] = v
            r = 0
            for k, i in enumerate(order[engname]):
                o = ops[i]
                while o['region'] > r:
                    for f in engs:
                        if f != engname:
                            wait(f, esem[f], ecnt_at[r][f])
                    for s_, v in scnt_at[r].items():
                        wait('d_' + s_, ssem[s_], v)
                    r += 1
                for d in o['deps']:
                    p = ops[d]
                    if p['slot'] is not None:
                        wait('d_' + p['slot'], ssem[p['slot']], p['val'])
                    else:
                        wait(p['eng'], esem[p['eng']], p['val'])
                ins = o['fn'](e)
                if o['slot'] is not None:
                    ins.then_inc(ssem[o['slot']], 16)
                elif o['used']:
                    ins.then_inc(esem[engname], 1)

        @block.tensor
        def _(e):
            run('pe', e)

        @block.scalar
        def _(e):
            run('act', e)

        @block.vector
        def _(e):
            run('dve', e)

        @block.gpsimd
        def _(e):
            run('pool', e)

        @block.sync
        def _(e):
            run('sp', e)


def build_program(debug_phase=99, tile_sel=None):
    nc = bass.Bass("TRN2", target_bir_lowering=False)
    S = Sched()
    es = contextlib.ExitStack()

    def din(name, shape):
        return nc.dram_tensor(name, list(shape), F32, kind="ExternalInput").ap()

    def dout(name, shape):
        return nc.dram_tensor(name, list(shape), F32, kind="ExternalOutput").ap()

    def dint(name, shape):
        return nc.dram_tensor(name, list(shape), F32, kind="Internal").ap()

    xp = din("xp", [NPT, D])
    xs = din("xs", [NST, D])
    st_h = din("st_h", [SSEQ_PER_CORE, DL])
    st_lc = din("st_lc", [SSEQ_PER_CORE * 3, DL])
    st_cm = din("st_cm", [SSEQ_PER_CORE * 30, DL])
    cvec = din("cvec", [NSEQ, D])
    w_ada = din("w_ada", [D, 9 * D])
    b_ada = din("b_ada", [1, 9 * D])
    g3 = din("g3", [4, D])
    fgain = din("fgain", [1, D])
    wg1 = din("wg1", [D, DFF]); wu1 = din("wu1", [D, DFF]); wd1 = din("wd1", [DFF, D])
    wg2 = din("wg2", [D, DFF]); wu2 = din("wu2", [D, DFF]); wd2 = din("wd2", [DFF, D])
    w_in = din("w_in", [D, 2 * D])
    w_out = din("w_out", [D, D])
    v5 = din("v5", [44, DL])
    wa = din("wa", [8 * 64, 64])
    wx = din("wx", [8 * 64, 64])

    yp = dout("yp", [NPT, D])
    ys = dout("ys", [NST, D])
    o_h = dout("o_h", [NSEQ, DL])
    o_lc = dout("o_lc", [NSEQ * 3, DL])
    o_cm = dout("o_cm", [NSEQ * 30, DL])

    x1p = dint("x1p", [NPT, D]); x1s = dint("x1s", [NST, D])
    x2p = dint("x2p", [NPT, D]); x2s = dint("x2s", [NST, D])
    modS = dint("modS", [NSEQ, 9 * D])

    tiles = []
    for s in range(PSEQ_PER_CORE):
        for j in range(SEQ // NT):
            tiles.append(dict(kind='p', nt=NT, nb=NT // 128, segs=[(s, 0, NT)], row0=s * SEQ + j * NT,
                              first=(j == 0), last=(j == SEQ // NT - 1)))
    tiles.append(dict(kind='s', nt=NST, nb=1, segs=[(4 + q, q * DSEQ, DSEQ) for q in range(4)], row0=0,
                      first=True, last=True))

    if tile_sel is not None:
        tiles = [tiles[i] for i in tile_sel]

    def rows(t, b, tp, ts):
        base = tp if t['kind'] == 'p' else ts
        r0 = t['row0'] + b * 128
        return base[r0:r0 + 128, :]

    used = {}

    def alloc(stack, name, shape, dt):
        nb_ = int(np.prod(shape[1:])) * (2 if dt == BF16 else 4)
        used[name] = nb_
        return stack.enter_context(nc.sbuf_tensor(name, list(shape), dt))
    build_program.used = used

    PS = [es.enter_context(nc.psum_tensor("ps%d" % i, [128, 512], F32)) for i in range(8)]

    PSB = [PS[i][:].bitcast(BF16) for i in range(2)]
    ident = alloc(es, "ident", [128, 128], F32)
    identb = alloc(es, "identb", [128, 128], BF16)
    AB_A = alloc(es, "AB_A", [128, 3, NSEQ, 8], F32)
    AB_B = alloc(es, "AB_B", [128, 3, NSEQ, 8], F32)
    small = alloc(es, "small", [128, 64], F32)

    identd = din("identd", [128, 128])
    S.add('sp', lambda e: e.dma_start(out=ident[:], in_=identd[:, :]), writes=['ident'], slot='ci')
    S.add('dve', lambda e: e.tensor_copy(out=identb[:], in_=ident[:]), reads=['ident'], writes=['identb'])

    with contextlib.ExitStack() as p0:
        c_sb = alloc(p0, "c_sb", [NSEQ, D], F32)
        cs_sb = alloc(p0, "cs_sb", [NSEQ, D], F32)
        cT = alloc(p0, "cT", [128, 8, NSEQ], BF16)
        modsb = alloc(p0, "modsb", [NSEQ, 9 * D], F32)
        bada = alloc(p0, "bada", [NSEQ, 9 * D], F32)
        g3sb = alloc(p0, "g3sb", [4, D], F32)
        g3T = alloc(p0, "g3T", [128, 8, 4], F32)
        modT = alloc(p0, "modT", [128, 48, NSEQ], F32)
        wslab = [alloc(p0, "wslab%d" % i, [128, 8, 512], BF16) for i in range(3)]

        S.add('sp', lambda e: e.dma_start(out=c_sb[:], in_=cvec[:, :]), writes=['c_sb'], slot='cA')
        S.add('sp', lambda e: e.dma_start(out=bada[:], in_=b_ada[0, :].partition_broadcast(NSEQ)),
              writes=['bada'], slot='cB')
        S.add('sp', lambda e: e.dma_start(out=g3sb[:], in_=g3[:, :]), writes=['g3sb'], slot='cC')
        S.add('act', lambda e: e.activation(out=cs_sb[:], in_=c_sb[:], func=AF.Tanh, scale=0.5),
              reads=['c_sb'], writes=['cs_sb'])
        S.add('dve', lambda e: e.scalar_tensor_tensor(out=cs_sb[:], in0=cs_sb[:], scalar=1.0, in1=c_sb[:],
                                                      op0=ALU.add, op1=ALU.mult),
              reads=['cs_sb', 'c_sb'], writes=['cs_sb'])
        S.add('dve', lambda e: e.tensor_scalar(out=cs_sb[:], in0=cs_sb[:], scalar1=0.5, scalar2=None, op0=ALU.mult),
              reads=['cs_sb'], writes=['cs_sb'])
        for k in range(8):
            S.add('pe', lambda e, k=k: e.transpose(out=PS[0][:, k * 8:(k + 1) * 8], in_=cs_sb[:, k * 128:(k + 1) * 128],
                                                   identity=ident[0:NSEQ, 0:NSEQ]),
                  reads=['cs_sb', 'ident'], writes=['ps0'])
        S.add('dve', lambda e: e.tensor_copy(out=cT[:].rearrange("p k s -> p (k s)"), in_=PS[0][:, 0:64]),
              reads=['ps0'], writes=['cT'])
        for k in range(8):
            S.add('pe', lambda e, k=k: e.transpose(out=PS[1][:, k * 4:(k + 1) * 4], in_=g3sb[:, k * 128:(k + 1) * 128],
                                                   identity=ident[0:4, 0:4]),
                  reads=['g3sb', 'ident'], writes=['ps1'])
        S.add('dve', lambda e: e.tensor_copy(out=g3T[:].rearrange("p k s -> p (k s)"), in_=PS[1][:, 0:32]),
              reads=['ps1'], writes=['g3T'])
        for sl in range(18):
            wb = wslab[sl % 3]
            wn = "wslab%d" % (sl % 3)
            S.add('pool', lambda e, sl=sl, wb=wb: e.dma_start(
                out=wb[:], in_=w_ada[:, sl * 512:(sl + 1) * 512].rearrange("(k p) n -> p k n", p=128)),
                writes=[wn], slot='wsl%d' % (sl % 3))
            bank = 2 + (sl % 2)
            for k in range(8):
                S.add('pe', lambda e, k=k, wb=wb, bank=bank: e.matmul(PS[bank][0:NSEQ, :], lhsT=cT[:, k, :], rhs=wb[:, k, :],
                                                                      start=(k == 0), stop=(k == 7)),
                      reads=['cT', wn], writes=['ps%d' % bank])
            S.add('dve', lambda e, sl=sl, bank=bank: e.tensor_tensor(
                out=modsb[:, sl * 512:(sl + 1) * 512], in0=PS[bank][0:NSEQ, :], in1=bada[:, sl * 512:(sl + 1) * 512],
                op=ALU.add), reads=['ps%d' % bank, 'bada'], writes=['modsb'])
        for kk in (2, 8):
            S.add('dve', lambda e, kk=kk: e.tensor_scalar(out=modsb[:, kk * D:(kk + 1) * D], in0=modsb[:, kk * D:(kk + 1) * D],
                                                          scalar1=0.5, scalar2=None, op0=ALU.mult),
                  reads=['modsb'], writes=['modsb'])
        S.add('sp', lambda e: e.dma_start(out=modS[:, :], in_=modsb[:]), reads=['modsb'], writes=['modS'], slot='c1')
        kinds = [0, 1, 3, 4, 6, 7]
        for qi, kk in enumerate(kinds):
            for k in range(8):
                col = (qi * 8 + k) * NSEQ
                bank = 4 + (qi * 8 + k) // 32
                S.add('pe', lambda e, kk=kk, k=k, col=col: e.transpose(
                    out=PS[4][:, col:col + NSEQ], in_=modsb[:, kk * D + k * 128: kk * D + (k + 1) * 128],
                    identity=ident[0:NSEQ, 0:NSEQ]), reads=['modsb', 'ident'], writes=['ps4'])
        S.add('dve', lambda e: e.tensor_copy(out=modT[:].rearrange("p a s -> p (a s)"), in_=PS[4][:, 0:48 * NSEQ]),
              reads=['ps4'], writes=['modT'])
        for k3 in range(3):
            for s in range(NSEQ):
                S.add('dve', lambda e, k3=k3, s=s: e.scalar_tensor_tensor(
                    out=AB_A[:, k3, s, :], in0=modT[:, (2 * k3 + 1) * 8:(2 * k3 + 2) * 8, s], scalar=1.0,
                    in1=g3T[:, :, k3], op0=ALU.add, op1=ALU.mult), reads=['modT', 'g3T'], writes=['AB'])
                S.add('dve', lambda e, k3=k3, s=s: e.tensor_copy(
                    out=AB_B[:, k3, s, :], in_=modT[:, (2 * k3) * 8:(2 * k3 + 1) * 8, s]), reads=['modT'], writes=['AB'])
    S.barrier(reorder=False)

    def fe_load(t, src_p, src_s, xin, b):
        slot = b % len(xin)
        xb = xin[slot]
        xn = 'xin%d' % slot
        S.add('sp', lambda e, xb=xb, b=b: e.dma_start(out=xb[:], in_=rows(t, b, src_p, src_s)),
              writes=[xn], slot=xn)

    def fe_normb(t, xin, xnb, stat, b):
        slot = b % len(xin)
        xb = xin[slot]
        xn = 'xin%d' % slot
        xq = xnb[b]
        xqn = 'xnb%d' % b
        S.add('act', lambda e, xb=xb, b=b, xq=xq: e.activation(out=xq[:], in_=xb[:], func=AF.Square,
                                                               accum_out=stat[:, b:b + 1]),
              reads=[xn], writes=[xqn, 'stat%d' % b])
        S.add('act', lambda e, b=b: e.activation(out=stat[:, 4 + b:5 + b], in_=stat[:, b:b + 1], func=AF.Sqrt,
                                                 scale=1.0 / D, bias=epsc[:, 0:1]),
              reads=['stat%d' % b, 'epsc'], writes=['statr%d' % b])
        S.add('dve', lambda e, b=b: e.reciprocal(out=stat[:, 8 + b:9 + b], in_=stat[:, 4 + b:5 + b]),
              reads=['statr%d' % b], writes=['stati%d' % b])
        S.add('dve', lambda e, xb=xb, b=b, xq=xq: e.tensor_scalar(out=xq[:], in0=xb[:], scalar1=stat[:, 8 + b:9 + b],
                                                                  scalar2=None, op0=ALU.mult),
              reads=[xn, 'stati%d' % b, xqn], writes=[xqn])

    def fe_norm(t, src_p, src_s, xin, xnb, stat):
        nb = t['nb']
        ns = len(xin)
        for b in range(min(ns, nb)):
            fe_load(t, src_p, src_s, xin, b)
        for b in range(nb):
            fe_normb(t, xin, xnb, stat, b)
            if b + ns < nb:
                fe_load(t, src_p, src_s, xin, b + ns)

    def fe_tr(t, k3, hT, xnb, banks=(0, 1)):
        for b in range(t['nb']):
            xq = xnb[b]
            xqn = 'xnb%d' % b
            bank = banks[b % len(banks)]
            for k in range(8):
                S.add('pe', lambda e, xq=xq, k=k, bank=bank: e.transpose(
                    out=PSB[bank][:, k * 128:(k + 1) * 128], in_=xq[:, k * 128:(k + 1) * 128], identity=identb[:]),
                    reads=[xqn, 'identb'], writes=['ps%d' % bank])
            for k in range(8):
                for (s, c0, L) in t['segs']:
                    lo = max(c0, b * 128)
                    hi = min(c0 + L, (b + 1) * 128)
                    if lo >= hi:
                        continue
                    pl = k * 128 + (lo - b * 128)
                    if k % 2 == 0:
                        S.add('act', lambda e, k=k, s=s, lo=lo, hi=hi, pl=pl, bank=bank: e.activation(
                            out=hT[:, k, lo:hi], in_=PSB[bank][:, pl:pl + hi - lo], func=AF.Identity,
                            scale=AB_A[:, k3, s, k:k + 1], bias=AB_B[:, k3, s, k:k + 1]),
                            reads=['ps%d' % bank, 'AB'], writes=['hT'])
                    else:
                        S.add('dve', lambda e, k=k, s=s, lo=lo, hi=hi, pl=pl, bank=bank: e.tensor_scalar(
                            out=hT[:, k, lo:hi], in0=PSB[bank][:, pl:pl + hi - lo],
                            scalar1=AB_A[:, k3, s, k:k + 1], scalar2=AB_B[:, k3, s, k:k + 1],
                            op0=ALU.mult, op1=ALU.add),
                            reads=['ps%d' % bank, 'AB'], writes=['hT'])

    def load_gate(t, k9, gate, gslot):
        gn = 'gate%d' % gslot
        for (s, c0, L) in t['segs']:
            if t['kind'] == 'p':
                S.add('sp', lambda e, s=s: e.dma_start(out=gate[gslot][:],
                                                       in_=modS[s, k9 * D:(k9 + 1) * D].partition_broadcast(128)),
                      reads=['modS'], writes=[gn], slot=gn)
            else:
                S.add('sp', lambda e, s=s, c0=c0, L=L: e.dma_start(
                    out=gate[gslot][c0:c0 + L, :], in_=modS[s, k9 * D:(k9 + 1) * D].partition_broadcast(L)),
                    reads=['modS'], writes=[gn], slot=gn)

    epsc = alloc(es, "epsc", [128, 4], F32)
    S.add('dve', lambda e: e.memset(epsc[:, 0:1], EPS), writes=['epsc'])
    S.add('dve', lambda e: e.memset(epsc[:, 1:2], 4.0 * EPS), writes=['epsc'])
    S.add('dve', lambda e: e.memset(epsc[:, 2:3], 1.0), writes=['epsc'])
    S.add('dve', lambda e: e.memset(epsc[:, 3:4], 0.25), writes=['epsc'])

    def ffn_phase(pi, src_p, src_s, dst_p, dst_s, k3, k9, wg, wu, wd, final, fence):
        with contextlib.ExitStack() as ph:
            Wg = alloc(ph, "Wg%d" % pi, [128, 8, DFF], BF16)
            Wu = alloc(ph, "Wu%d" % pi, [128, 8, DFF], BF16)
            Wd = alloc(ph, "Wd%d" % pi, [128, NJ, D], BF16)
            hT = alloc(ph, "hT%d" % pi, [128, 8, NT], BF16)
            aT = alloc(ph, "aT%d" % pi, [128, NJ, NT], BF16)
            xin = [alloc(ph, "xin%d_%d" % (pi, i), [128, D], F32) for i in range(2)]
            xep = [alloc(ph, "xep%d_%d" % (pi, i), [128, D], F32) for i in range(2)]
            xnb = [alloc(ph, "xnb%d_%d" % (pi, i), [128, D], BF16) for i in range(4)]
            sg = [alloc(ph, "sg%d_%d" % (pi, i), [128, NT], BF16) for i in range(2)]
            tmp = [alloc(ph, "tmp%d_%d" % (pi, i), [128, 512], F32) for i in range(2)]
            gate = [alloc(ph, "gate%d_%d" % (pi, i), [128, D], F32) for i in range(2)]
            stat = alloc(ph, "stat%d" % pi, [128, 16], F32)
            fg = alloc(ph, "fg%d" % pi, [128, D], F32) if final else None
            allres = ['Wg', 'Wu', 'Wd', 'hT', 'aT', 'xin0', 'xin1', 'xep0', 'xep1', 'xep2', 'sq', 'sg0', 'sg1',
                      'tmp0', 'tmp1', 'gate0', 'gate1', 'fg'] + ['stat%d' % b for b in range(4)] + \
                     ['statr%d' % b for b in range(4)] + ['stati%d' % b for b in range(4)] + ['fstat']
            NG = (NJ + 3) // 4
            wtk = [0]

            def tok():
                wtk[0] += 1
                return ['wtok%d' % (wtk[0] % 3)]
            for g in range(NG):
                c0_, c1_ = g * 512, min(DFF, (g + 1) * 512)
                S.add('pool', lambda e, c0_=c0_, c1_=c1_: e.dma_start(
                    out=Wg[:, :, c0_:c1_], in_=wg[:, c0_:c1_].rearrange("(k p) n -> p k n", p=128)),
                    writes=['Wg%d' % g] + tok(), slot='wG%d' % g)
                S.add('pool', lambda e, c0_=c0_, c1_=c1_: e.dma_start(
                    out=Wu[:, :, c0_:c1_], in_=wu[:, c0_:c1_].rearrange("(k p) n -> p k n", p=128)),
                    writes=['Wu%d' % g] + tok(), slot='wU%d' % g)
            for j in range(NJ):
                S.add('pool', lambda e, j=j: e.dma_start(out=Wd[:, j, :], in_=wd[j * 128:(j + 1) * 128, :]),
                      writes=['Wd%d' % j] + tok(), slot='wD%d' % j)
            if final:
                S.add('sp', lambda e: e.dma_start(out=fg[:], in_=fgain[0, :].partition_broadcast(128)),
                      writes=['fg'], slot='cD')
            gslot_of_tile = []
            new_gate = []
            gcount = 0
            prev_key = None
            for ti, t in enumerate(tiles):
                key = (t['kind'], t['segs'][0][0])
                if key != prev_key:
                    gcount += 1
                    new_gate.append(True)
                else:
                    new_gate.append(False)
                prev_key = key
                gslot_of_tile.append((gcount - 1) % 2)

            def gu(ti, t):
                nt = t['nt']
                if new_gate[ti]:
                    load_gate(t, k9, gate, gslot_of_tile[ti])
                tn = tiles[ti + 1] if ti + 1 < len(tiles) else None
                for j in range(NJ):
                    if tn is not None:
                        if j == 0:
                            for b in range(min(2, tn['nb'])):
                                fe_load(tn, src_p, src_s, xin, b)
                        if j == 4:
                            for b in range(min(2, tn['nb'])):
                                fe_normb(tn, xin, xnb, stat, b)
                                if b + 2 < tn['nb']:
                                    fe_load(tn, src_p, src_s, xin, b + 2)
                        if j == 8:
                            for b in range(2, tn['nb']):
                                fe_normb(tn, xin, xnb, stat, b)
                    gb = 2 + 2 * (j % 2)
                    ub = gb + 1
                    for k in range(8):
                        S.add('pe', lambda e, j=j, k=k, gb=gb, nt=nt: e.matmul(
                            PS[gb][:, 0:nt], lhsT=Wg[:, k, j * 128:(j + 1) * 128], rhs=hT[:, k, 0:nt],
                            start=(k == 0), stop=(k == 7)), reads=['Wg%d' % (j // 4), 'hT'], writes=['ps%d' % gb])
                    for k in range(8):
                        S.add('pe', lambda e, j=j, k=k, ub=ub, nt=nt: e.matmul(
                            PS[ub][:, 0:nt], lhsT=Wu[:, k, j * 128:(j + 1) * 128], rhs=hT[:, k, 0:nt],
                            start=(k == 0), stop=(k == 7)), reads=['Wu%d' % (j // 4), 'hT'], writes=['ps%d' % ub])
                    sgi = j % 2
                    S.add('act', lambda e, gb=gb, sgi=sgi, nt=nt: e.activation(out=sg[sgi][:, 0:nt], in_=PS[gb][:, 0:nt],
                                                                               func=AF.Silu),
                          reads=['ps%d' % gb], writes=['sg%d' % sgi])
                    S.add('dve', lambda e, j=j, ub=ub, sgi=sgi, nt=nt: e.tensor_tensor(
                        out=aT[:, j, 0:nt], in0=PS[ub][:, 0:nt], in1=sg[sgi][:, 0:nt], op=ALU.mult),
                        reads=['ps%d' % ub, 'sg%d' % sgi], writes=['aT'])

            def down(ti, t):
                nb = t['nb']
                gs = gslot_of_tile[ti]

                def ld(b):
                    xs_ = b % 2
                    S.add('sp', lambda e, b=b, xs_=xs_: e.dma_start(out=xep[xs_][:], in_=rows(t, b, src_p, src_s)),
                          writes=['xep%d' % xs_], slot='xep%d' % xs_)
                for b in range(min(nb, 2)):
                    ld(b)
                for b in range(nb):
                    xs_ = b % 2
                    xe = xep[xs_]
                    for n in range(2):
                        yb = 6 + n
                        for j in range(NJ):
                            S.add('pe', lambda e, j=j, b=b, n=n, yb=yb: e.matmul(
                                PS[yb][:, :], lhsT=aT[:, j, b * 128:(b + 1) * 128], rhs=Wd[:, j, n * 512:(n + 1) * 512],
                                start=(j == 0), stop=(j == NJ - 1)), reads=['aT', 'Wd%d' % j], writes=['ps%d' % yb])
                        S.add('dve', lambda e, n=n, yb=yb, gs=gs: e.tensor_tensor(
                            out=tmp[n][:], in0=PS[yb][:, :], in1=gate[gs][:, n * 512:(n + 1) * 512], op=ALU.mult),
                            reads=['ps%d' % yb, 'gate%d' % gs], writes=['tmp%d' % n])
                        S.add('pool', lambda e, n=n, xe=xe: e.tensor_tensor(
                            out=xe[:, n * 512:(n + 1) * 512], in0=xe[:, n * 512:(n + 1) * 512], in1=tmp[n][:], op=ALU.add),
                            reads=['tmp%d' % n, 'xep%d' % xs_], writes=['xep%d' % xs_])
                    if final:
                        S.add('act', lambda e, xe=xe, b=b: e.activation(out=tmp[0][:], in_=xe[:, 0:512], func=AF.Square,
                                                                        accum_out=stat[:, 12:13]),
                              reads=['xep%d' % xs_], writes=['tmp0', 'fstat'])
                        S.add('act', lambda e, xe=xe, b=b: e.activation(out=tmp[1][:], in_=xe[:, 512:1024], func=AF.Square,
                                                                        accum_out=stat[:, 15:16]),
                              reads=['xep%d' % xs_], writes=['tmp1', 'fstat'])
                        S.add('dve', lambda e: e.tensor_tensor(out=stat[:, 12:13], in0=stat[:, 12:13], in1=stat[:, 15:16], op=ALU.add),
                              reads=['fstat'], writes=['fstat'])
                        S.add('act', lambda e: e.activation(out=stat[:, 13:14], in_=stat[:, 12:13], func=AF.Sqrt,
                                                            scale=1.0 / D, bias=epsc[:, 0:1]),
                              reads=['fstat', 'epsc'], writes=['fstat'])
                        S.add('dve', lambda e: e.reciprocal(out=stat[:, 14:15], in_=stat[:, 13:14]),
                              reads=['fstat'], writes=['fstat'])
                        S.add('dve', lambda e, xe=xe: e.scalar_tensor_tensor(
                            out=xe[:], in0=xe[:], scalar=stat[:, 14:15], in1=fg[:], op0=ALU.mult, op1=ALU.mult),
                            reads=['xep%d' % xs_, 'fstat', 'fg'], writes=['xep%d' % xs_])
                    S.add('sp', lambda e, b=b, xe=xe: e.dma_start(out=rows(t, b, dst_p, dst_s), in_=xe[:]),
                          reads=['xep%d' % xs_], writes=['dst%d' % pi], slot='st%d' % xs_)
                    if b + 2 < nb:
                        ld(b + 2)

            fe_norm(tiles[0], src_p, src_s, xin, xnb, stat)
            fe_tr(tiles[0], k3, hT, xnb)
            for ti, t in enumerate(tiles):
                gu(ti, t)
                if ti + 1 < len(tiles):
                    fe_tr(tiles[ti + 1], k3, hT, xnb)
                down(ti, t)
        S.barrier()

    def mixer_phase(src_p, src_s, dst_p, dst_s):
        k3, k9 = 1, 5
        with contextlib.ExitStack() as ph:
            Win = alloc(ph, "Win", [128, 8, 2 * D], BF16)
            Wout = alloc(ph, "Wout", [128, 8, D], BF16)
            D31 = alloc(ph, "D31", [128, 4, 31, 128], BF16)
            Wab = alloc(ph, "Wab", [128, 2, 4, 128], BF16)
            ones = alloc(ph, "ones", [128, 128], BF16)
            V5T = alloc(ph, "V5T", [128, 4, 44], F32)
            der = alloc(ph, "der", [128, 4, 4], F32)
            hT = alloc(ph, "hTm", [128, 8, NT], BF16)
            xin = [alloc(ph, "xinm%d" % i, [128, D], F32) for i in range(1)]
            xep = [alloc(ph, "xepm%d" % i, [128, D], F32) for i in range(2)]
            xnb = [alloc(ph, "xnbm%d" % i, [128, D], BF16) for i in range(4)]
            stat = alloc(ph, "statm", [128, 16], F32)
            tmp = [alloc(ph, "tmpm%d" % i, [128, 512], F32) for i in range(1)]
            gate = [alloc(ph, "gatem0", [128, D], F32)]
            LXW = 3 + NT
            CUW = 30 + NT
            LX = alloc(ph, "LX", [128, 4, LXW], F32)
            CU = alloc(ph, "CU", [128, 4, CUW], BF16)
            BA = alloc(ph, "BA", [128, 4, NT], F32)
            BB = alloc(ph, "BB", [128, 4, NT], F32)
            BC = alloc(ph, "BC", [128, 4, NT], F32)
            BD = alloc(ph, "BD", [128, 4, NT], F32)
            BE = alloc(ph, "BE", [128, 4, NT], F32)
            YLb = [alloc(ph, "YLb%d" % i, [128, NT], BF16) for i in range(2)]
            lsq = alloc(ph, "lsq", [128, NT], BF16)
            cb = [alloc(ph, "cb%d" % i, [128, NT], BF16) for i in range(2)]
            tA = [alloc(ph, "tA%d" % i, [128, NT], F32) for i in range(2)]
            tB = [alloc(ph, "tB%d" % i, [128, NT], F32) for i in range(2)]
            tC = alloc(ph, "tC", [128, NT], F32)
            RS = [alloc(ph, "RS%d" % i, [128, NT], F32) for i in range(3)]
            hst = alloc(ph, "hst", [128, 4, 4], F32)
            CMH = alloc(ph, "CMH", [128, 4, 120], F32)
            LCH = alloc(ph, "LCH", [128, 4, 16], F32)
            CMO = alloc(ph, "CMO", [128, 4, 240], F32)
            LCO = alloc(ph, "LCO", [128, 4, 24], F32)
            LHO = alloc(ph, "LHO", [128, 4, 8], F32)
            S1 = tA[0][0:120, :]
            S2 = tB[1][0:16, :]
            V5 = tB[0][0:44, :]
            ostg = tA[0][0:120, :]
            MT = alloc(ph, "MTm", [128, 8, NT], BF16)

            for k in range(8):
                S.add('pool', lambda e, k=k: e.dma_start(out=Win[:, k, :], in_=w_in[k * 128:(k + 1) * 128, :]),
                      writes=['Win'], slot='wA')
            for k in range(8):
                S.add('pool', lambda e, k=k: e.dma_start(out=Wout[:, k, :], in_=w_out[k * 128:(k + 1) * 128, :]),
                      writes=['Wout'], slot='wB')
            S.add('dve', lambda e: e.memset(Wab[:].rearrange("p a c m -> p (a c m)"), 0.0), writes=['Wab'])
            S.add('dve', lambda e: e.memset(ones[:], 1.0), writes=['ones'])
            for wi, wsrc in enumerate((wa, wx)):
                for c in range(4):
                    for hh in range(2):
                        S.add('pool', lambda e, wi=wi, wsrc=wsrc, c=c, hh=hh: e.dma_start(
                            out=Wab[hh * 64:(hh + 1) * 64, wi, c, hh * 64:(hh + 1) * 64],
                            in_=wsrc[(2 * c + hh) * 64:(2 * c + hh + 1) * 64, :]), reads=['Wab'], writes=['Wab'], slot='wC')
            S.add('sp', lambda e: e.dma_start(out=V5, in_=v5[:, :]), writes=['tB0'], slot='cA')
            S.add('sp', lambda e: e.dma_start(out=S1, in_=st_cm[:, :]), writes=['tA0'], slot='cB')
            S.add('sp', lambda e: e.dma_start(out=S2[0:12, :], in_=st_lc[:, :]), writes=['tB1'], slot='cC')
            S.add('sp', lambda e: e.dma_start(out=S2[12:16, :], in_=st_h[:, :]), writes=['tB1'], slot='cC')
            for c in range(4):
                S.add('pe', lambda e, c=c: e.transpose(out=PS[2][:, c * 44:(c + 1) * 44], in_=V5[:, c * 128:(c + 1) * 128],
                                                       identity=ident[0:44, 0:44]), reads=['tB0', 'ident'], writes=['ps2'])
                S.add('pe', lambda e, c=c: e.transpose(out=PS[3][:, c * 120:(c + 1) * 120], in_=S1[:, c * 128:(c + 1) * 128],
                                                       identity=ident[0:120, 0:120]), reads=['tA0', 'ident'], writes=['ps3'])
                S.add('pe', lambda e, c=c: e.transpose(out=PS[4][:, c * 16:(c + 1) * 16], in_=S2[:, c * 128:(c + 1) * 128],
                                                       identity=ident[0:16, 0:16]), reads=['tB1', 'ident'], writes=['ps4'])
            S.add('dve', lambda e: e.tensor_copy(out=V5T[:].rearrange("p c v -> p (c v)"), in_=PS[2][:, 0:176]),
                  reads=['ps2'], writes=['V5T'])
            S.add('dve', lambda e: e.tensor_copy(out=CMH[:].rearrange("p c v -> p (c v)"), in_=PS[3][:, 0:480]),
                  reads=['ps3'], writes=['CMH'])
            S.add('dve', lambda e: e.tensor_copy(out=LCH[:].rearrange("p c v -> p (c v)"), in_=PS[4][:, 0:64]),
                  reads=['ps4'], writes=['LCH'])
            S.add('dve', lambda e: e.tensor_scalar(out=der[:, :, 0], in0=V5T[:, :, 5], scalar1=0.5, scalar2=None, op0=ALU.mult),
                  reads=['V5T'], writes=['der'])
            S.add('dve', lambda e: e.tensor_scalar(out=der[:, :, 1], in0=V5T[:, :, 6], scalar1=0.5, scalar2=None, op0=ALU.mult),
                  reads=['V5T'], writes=['der'])
            S.add('act', lambda e: e.activation(out=der[:, :, 2], in_=V5T[:, :, 7], func=AF.Exp, scale=-1.0),
                  reads=['V5T', 'der'], writes=['der'])
            S.add('act', lambda e: e.activation(out=der[:, :, 2], in_=der[:, :, 2], func=AF.Ln, bias=epsc[:, 2:3]),
                  reads=['der', 'epsc'], writes=['der'])
            S.add('dve', lambda e: e.tensor_scalar(out=der[:, :, 3], in0=der[:, :, 2], scalar1=-4.0, scalar2=None, op0=ALU.mult),
                  reads=['der'], writes=['der'])
            S.add('dve', lambda e: e.tensor_scalar(out=der[:, :, 2], in0=der[:, :, 2], scalar1=-8.0, scalar2=None, op0=ALU.mult),
                  reads=['der'], writes=['der'])
            for c in range(4):
                for k in range(31):
                    S.add('pool' if k % 3 == 0 else 'dve', lambda e, c=c, k=k: e.tensor_scalar(
                        out=D31[:, c, k, :], in0=ident[:], scalar1=V5T[:, c, 8 + k:9 + k], scalar2=0.5,
                        op0=ALU.mult, op1=ALU.mult), reads=['ident', 'V5T'], writes=['D31_%d_%d' % (c, k)])
            S.add('pool', lambda e: e.memset(CMO[:].rearrange("p c v -> p (c v)"), 0.0), writes=['CMO'])
            S.add('pool', lambda e: e.memset(LCO[:].rearrange("p c v -> p (c v)"), 0.0), writes=['LCO'])
            S.add('pool', lambda e: e.memset(LHO[:].rearrange("p c v -> p (c v)"), 0.0), writes=['LHO'])

            def vs(c, i):
                return V5T[:, c, i:i + 1]

            new_gate = []
            prev_key = None
            for ti, t in enumerate(tiles):
                key = (t['kind'], t['segs'][0][0])
                new_gate.append(key != prev_key)
                prev_key = key

            def mix_tile(ti, t):
                nt = t['nt']
                nb = t['nb']
                segs = [(sq_, c0, L, j, j * (3 + L), j * (30 + L)) for j, (sq_, c0, L) in enumerate(t['segs'])]
                gs = 0
                tn = tiles[ti + 1] if ti + 1 < len(tiles) else None
                if new_gate[ti]:
                    load_gate(t, k9, gate, gs)

                def ld(b):
                    xs_ = b % 2
                    S.add('sp', lambda e, b=b, xs_=xs_: e.dma_start(out=xep[xs_][:], in_=rows(t, b, src_p, src_s)),
                          writes=['xep%d' % xs_], slot='xep%d' % xs_)
                for b in range(min(nb, 2)):
                    ld(b)
                if t['kind'] == 'p':
                    if t['first']:
                        S.add('pool', lambda e: e.memset(LX[:, :, 0:3], 0.0), writes=['LX'])
                        S.add('pool', lambda e: e.memset(CU[:, :, 0:30], 0.0), writes=['CU'])
                        S.add('pool', lambda e: e.memset(hst[:].rearrange("p c j -> p (c j)"), 0.0), writes=['hst'])
                    else:
                        S.add('pool', lambda e: e.tensor_copy(out=LX[:, :, 0:3], in_=LX[:, :, NT:NT + 3]),
                              reads=['LX'], writes=['LX'])
                        S.add('pool', lambda e: e.tensor_copy(out=CU[:, :, 0:30], in_=CU[:, :, NT:NT + 30]),
                              reads=['CU'], writes=['CU'])
                else:
                    for (sq_, c0, L, j, lxo, cuo) in segs:
                        S.add('pool', lambda e, j=j, lxo=lxo: e.tensor_copy(out=LX[:, :, lxo:lxo + 3], in_=LCH[:, :, j * 3:j * 3 + 3]),
                              reads=['LX', 'LCH'], writes=['LX'])
                        S.add('pool', lambda e, j=j, cuo=cuo: e.tensor_scalar(
                            out=CU[:, :, cuo:cuo + 30], in0=CMH[:, :, j * 30:j * 30 + 30], scalar1=2.0, scalar2=None,
                            op0=ALU.mult), reads=['CU', 'CMH'], writes=['CU'])
                        S.add('pool', lambda e, j=j: e.tensor_copy(out=hst[:, :, j], in_=LCH[:, :, 12 + j]),
                              reads=['LCH', 'hst'], writes=['hst'])
                fe_tr(t, k3, hT, xnb, banks=(0,))

                Lq = []
                Cq = []

                def la(*a, **kw):
                    Lq.append((a, kw))

                def ca(*a, **kw):
                    Cq.append((a, kw))

                def zmm(add, m, bank):
                    for k in range(8):
                        add('pe', lambda e, m=m, k=k, bank=bank: e.matmul(
                            PS[bank][:, 0:nt], lhsT=Win[:, k, m * 128:(m + 1) * 128], rhs=hT[:, k, 0:nt],
                            start=(k == 0), stop=(k == 7)), reads=['Win', 'hT'], writes=['ps%d' % bank])

                for c in range(4):
                    bank = 2 + (c % 2)
                    zmm(la, c, bank)
                    for (sq_, c0, L, j, lxo, cuo) in segs:
                        la('act', lambda e, c=c, bank=bank, c0=c0, L=L, lxo=lxo: e.activation(
                            out=LX[:, c, lxo + 3:lxo + 3 + L], in_=PS[bank][:, c0:c0 + L], func=AF.Copy),
                            reads=['ps%d' % bank, 'LX'], writes=['LX'])
                    for (sq_, c0, L, j, lxo, cuo) in segs:
                        la('dve', lambda e, c=c, c0=c0, L=L, lxo=lxo: e.tensor_scalar(
                            out=BA[:, c, c0:c0 + L], in0=LX[:, c, lxo:lxo + L], scalar1=vs(c, 0), scalar2=vs(c, 4),
                            op0=ALU.mult, op1=ALU.add), reads=['LX', 'V5T'], writes=['BA%d' % c])
                        for k in range(1, 4):
                            la('dve', lambda e, c=c, c0=c0, L=L, lxo=lxo, k=k: e.scalar_tensor_tensor(
                                out=BA[:, c, c0:c0 + L], in0=LX[:, c, lxo + k:lxo + k + L], scalar=vs(c, k),
                                in1=BA[:, c, c0:c0 + L], op0=ALU.mult, op1=ALU.add), reads=['LX', 'V5T', 'BA%d' % c], writes=['BA%d' % c])
                    yb = YLb[c % 2]
                    ybn = 'YLb%d' % (c % 2)
                    la('pool', lambda e, c=c, yb=yb: e.tensor_copy(out=yb[:, 0:nt], in_=BA[:, c, 0:nt]),
                       reads=['BA%d' % c], writes=[ybn])
                    bank2 = 2 + ((c + 1) % 2)
                    zmm(la, 4 + c, bank2)
                    la('dve', lambda e, c=c, bank2=bank2: e.tensor_copy(out=BB[:, c, 0:nt], in_=PS[bank2][:, 0:nt]),
                       reads=['ps%d' % bank2], writes=['BB%d' % c])
                    ra, ix = 2 + (c % 2), 2 + ((c + 1) % 2)
                    la('pe', lambda e, c=c, yb=yb, ra=ra: e.matmul(PS[ra][:, 0:nt], lhsT=Wab[:, 0, c, :], rhs=yb[:, 0:nt],
                                                                  start=True, stop=True), reads=['Wab', ybn], writes=['ps%d' % ra])
                    la('pe', lambda e, c=c, yb=yb, ix=ix: e.matmul(PS[ix][:, 0:nt], lhsT=Wab[:, 1, c, :], rhs=yb[:, 0:nt],
                                                                  start=True, stop=True), reads=['Wab', ybn], writes=['ps%d' % ix])
                    tr = tA[c % 2]
                    trn = 'tA%d' % (c % 2)
                    ti_ = tB[c % 2]
                    tin = 'tB%d' % (c % 2)
                    la('act', lambda e, c=c, ra=ra, tr=tr: e.activation(out=tr[:, 0:nt], in_=PS[ra][:, 0:nt], func=AF.Tanh,
                                                                       scale=0.5, bias=der[:, c, 0:1]),
                       reads=['ps%d' % ra, 'der'], writes=[trn])
                    la('act', lambda e, c=c, ix=ix, ti_=ti_: e.activation(out=ti_[:, 0:nt], in_=PS[ix][:, 0:nt], func=AF.Tanh,
                                                                         scale=0.5, bias=der[:, c, 1:2]),
                       reads=['ps%d' % ix, 'der'], writes=[tin])
                    la('act', lambda e, c=c, tr=tr: e.activation(out=BD[:, c, 0:nt], in_=tr[:, 0:nt], func=AF.Exp,
                                                                 scale=der[:, c, 3:4], bias=der[:, c, 3:4]),
                       reads=[trn, 'der'], writes=['BD%d' % c])
                    la('act', lambda e, c=c, tr=tr: e.activation(out=BC[:, c, 0:nt], in_=tr[:, 0:nt], func=AF.Exp,
                                                                 scale=der[:, c, 2:3], bias=der[:, c, 2:3]),
                       reads=[trn, 'der'], writes=['BC%d' % c])
                    la('dve', lambda e, c=c, ti_=ti_: e.scalar_tensor_tensor(
                        out=BA[:, c, 0:nt], in0=ti_[:, 0:nt], scalar=1.0, in1=BA[:, c, 0:nt], op0=ALU.add, op1=ALU.mult),
                        reads=[tin, 'BA%d' % c], writes=['BA%d' % c])
                    tq = tB[c % 2]
                    tqn = 'tB%d' % (c % 2)
                    la('act', lambda e, c=c, tq=tq: e.activation(out=tq[:, 0:nt], in_=BB[:, c, 0:nt], func=AF.Square),
                       reads=['BB%d' % c], writes=[tqn])
                    la('dve', lambda e, tq=tq: e.tensor_scalar(out=tq[:, 0:nt], in0=tq[:, 0:nt], scalar1=0.044715, scalar2=1.0,
                                                               op0=ALU.mult, op1=ALU.add), reads=[tqn], writes=[tqn])
                    la('dve', lambda e, c=c, tq=tq: e.tensor_tensor(out=tq[:, 0:nt], in0=tq[:, 0:nt], in1=BB[:, c, 0:nt], op=ALU.mult),
                       reads=[tqn, 'BB%d' % c], writes=[tqn])
                    la('act', lambda e, tq=tq: e.activation(out=tq[:, 0:nt], in_=tq[:, 0:nt], func=AF.Tanh, scale=0.7978845608028654),
                       reads=[tqn], writes=[tqn])
                    la('dve', lambda e, c=c, tq=tq: e.scalar_tensor_tensor(
                        out=BB[:, c, 0:nt], in0=tq[:, 0:nt], scalar=1.0, in1=BB[:, c, 0:nt], op0=ALU.add, op1=ALU.mult),
                        reads=[tqn, 'BB%d' % c], writes=['BB%d' % c])
                if t['last']:
                    for (sq_, c0, L, j, lxo, cuo) in segs:
                        la('pool', lambda e, sq_=sq_, lxo=lxo, L=L: e.tensor_copy(
                            out=LCO[:, :, sq_ * 3:sq_ * 3 + 3], in_=LX[:, :, lxo + L:lxo + L + 3]),
                            reads=['LX'], writes=['LCO'])
                for c in range(4):
                    la('act', lambda e, c=c: e.activation(out=BC[:, c, 0:nt], in_=BC[:, c, 0:nt], func=AF.Sqrt,
                                                          scale=-0.25, bias=epsc[:, 3:4]),
                       reads=['BC%d' % c, 'epsc'], writes=['BC%d' % c])
                    la('pool', lambda e, c=c: e.tensor_tensor(out=BA[:, c, 0:nt], in0=BA[:, c, 0:nt], in1=BC[:, c, 0:nt], op=ALU.mult),
                       reads=['BA%d' % c, 'BC%d' % c], writes=['BA%d' % c])
                for c in range(4):
                    for (sq_, c0, L, j, lxo, cuo) in segs:
                        la('dve', lambda e, c=c, c0=c0, L=L, j=j: e.tensor_tensor_scan(
                            out=BC[:, c, c0:c0 + L], data0=BD[:, c, c0:c0 + L], data1=BA[:, c, c0:c0 + L],
                            initial=hst[:, c, j:j + 1], op0=ALU.mult, op1=ALU.add),
                            reads=['BD%d' % c, 'BA%d' % c, 'hst', 'BC%d' % c], writes=['BC%d' % c])
                    la('pool', lambda e, c=c: e.tensor_tensor(out=BD[:, c, 0:nt], in0=BC[:, c, 0:nt], in1=BB[:, c, 0:nt], op=ALU.mult),
                       reads=['BC%d' % c, 'BB%d' % c, 'BD%d' % c], writes=['BD%d' % c])
                    la('act', lambda e, c=c: e.activation(out=lsq[:, 0:nt], in_=BD[:, c, 0:nt], func=AF.Square),
                       reads=['BD%d' % c], writes=['lsq'])
                    la('pe', lambda e, c=c: e.matmul(PS[6][:, 0:nt], lhsT=ones[:], rhs=lsq[:, 0:nt],
                                                     start=(c == 0), stop=(c == 3)), reads=['ones', 'lsq'], writes=['ps6'])
                allBC = ['BC%d' % c for c in range(4)]
                for (sq_, c0, L, j, lxo, cuo) in segs:
                    la('pool', lambda e, c0=c0, L=L, j=j: e.tensor_copy(out=hst[:, :, j], in_=BC[:, :, c0 + L - 1]),
                       reads=allBC + ['hst'], writes=['hst'])
                    if t['last']:
                        la('pool', lambda e, c0=c0, L=L, sq_=sq_: e.tensor_copy(out=LHO[:, :, sq_], in_=BC[:, :, c0 + L - 1]),
                           reads=allBC, writes=['LHO'])
                la('act', lambda e: e.activation(out=RS[0][:, 0:nt], in_=PS[6][:, 0:nt], func=AF.Sqrt, scale=1.0 / DL,
                                                 bias=epsc[:, 1:2]), reads=['ps6', 'epsc'], writes=['RS0'])
                la('dve', lambda e: e.reciprocal(out=RS[0][:, 0:nt], in_=RS[0][:, 0:nt]), reads=['RS0'], writes=['RS0'])
                for c in range(4):
                    la('dve', lambda e, c=c: e.scalar_tensor_tensor(
                        out=MT[:, c, 0:nt], in0=BD[:, c, 0:nt], scalar=vs(c, 42), in1=RS[0][:, 0:nt], op0=ALU.mult, op1=ALU.mult),
                        reads=['BD%d' % c, 'V5T', 'RS0'], writes=['MT'])

                for c in range(4):
                    bg, bv = 4, 5
                    zmm(ca, 12 + c, bg)
                    zmm(ca, 8 + c, bv)
                    ca('act', lambda e, bg=bg: e.activation(out=tC[:, 0:nt], in_=PS[bg][:, 0:nt], func=AF.Tanh, scale=0.5),
                       reads=['ps%d' % bg], writes=['tC'])
                    for (sq_, c0, L, j, lxo, cuo) in segs:
                        ca('dve', lambda e, c=c, bv=bv, c0=c0, L=L, cuo=cuo: e.scalar_tensor_tensor(
                            out=CU[:, c, cuo + 30:cuo + 30 + L], in0=tC[:, c0:c0 + L], scalar=1.0, in1=PS[bv][:, c0:c0 + L],
                            op0=ALU.add, op1=ALU.mult), reads=['tC', 'ps%d' % bv, 'CU'], writes=['CU'])
                        if t['last']:
                            ca('dve', lambda e, c=c, bv=bv, c0=c0, L=L, sq_=sq_: e.scalar_tensor_tensor(
                                out=CMO[:, c, sq_ * 30:sq_ * 30 + 30], in0=tC[:, c0 + L - 30:c0 + L], scalar=1.0,
                                in1=PS[bv][:, c0 + L - 30:c0 + L], op0=ALU.add, op1=ALU.mult),
                                reads=['tC', 'ps%d' % bv, 'CMO'], writes=['CMO'])
                if t['last']:
                    for (sq_, c0, L, j, lxo, cuo) in segs:
                        ca('pool', lambda e, sq_=sq_: e.tensor_scalar(
                            out=CMO[:, :, sq_ * 30:sq_ * 30 + 30], in0=CMO[:, :, sq_ * 30:sq_ * 30 + 30], scalar1=0.5,
                            scalar2=None, op0=ALU.mult), reads=['CMO'], writes=['CMO'])
                for c in range(4):
                    bank = 4 + (c % 2)
                    for (sq_, c0, L, j, lxo, cuo) in segs:
                        for k in range(31):
                            ca('pe', lambda e, c=c, k=k, bank=bank, c0=c0, L=L, cuo=cuo: e.matmul(
                                PS[bank][:, c0:c0 + L], lhsT=D31[:, c, k, :], rhs=CU[:, c, cuo + k:cuo + k + L],
                                start=(k == 0), stop=(k == 30)), reads=['D31_%d_%d' % (c, k), 'CU'], writes=['ps%d' % bank])
                    ca('act', lambda e, c=c, bank=bank: e.activation(out=BE[:, c, 0:nt], in_=PS[bank][:, 0:nt], func=AF.Identity,
                                                                     bias=vs(c, 39)), reads=['ps%d' % bank, 'V5T'], writes=['BE%d' % c])
                    ca('pool', lambda e, c=c: e.tensor_copy(out=cb[0][:, 0:nt], in_=BE[:, c, 0:nt]), reads=['BE%d' % c], writes=['cb0'])
                    ca('act', lambda e, c=c: e.activation(out=cb[1][:, 0:nt], in_=BE[:, c, 0:nt], func=AF.Square),
                       reads=['BE%d' % c], writes=['cb1'])
                    ca('pe', lambda e, c=c: e.matmul(PS[7][:, 0:nt], lhsT=ones[:], rhs=cb[0][:, 0:nt],
                                                     start=(c == 0), stop=(c == 3)), reads=['ones', 'cb0'], writes=['ps7'])
                    ca('pe', lambda e, c=c: e.matmul(PS[1][:, 0:nt], lhsT=ones[:], rhs=cb[1][:, 0:nt],
                                                     start=(c == 0), stop=(c == 3)), reads=['ones', 'cb1'], writes=['ps1'])
                ca('act', lambda e: e.activation(out=RS[1][:, 0:nt], in_=PS[7][:, 0:nt], func=AF.Copy, scale=1.0 / DL),
                   reads=['ps7'], writes=['RS1'])
                ca('dve', lambda e: e.tensor_tensor(out=RS[2][:, 0:nt], in0=RS[1][:, 0:nt], in1=RS[1][:, 0:nt], op=ALU.mult),
                   reads=['RS1'], writes=['RS2'])
                ca('dve', lambda e: e.scalar_tensor_tensor(out=RS[2][:, 0:nt], in0=PS[1][:, 0:nt], scalar=1.0 / DL, in1=RS[2][:, 0:nt],
                                                           op0=ALU.mult, op1=ALU.subtract), reads=['ps1', 'RS2'], writes=['RS2'])
                ca('act', lambda e: e.activation(out=RS[2][:, 0:nt], in_=RS[2][:, 0:nt], func=AF.Sqrt, bias=epsc[:, 0:1]),
                   reads=['RS2', 'epsc'], writes=['RS2'])
                ca('dve', lambda e: e.reciprocal(out=RS[2][:, 0:nt], in_=RS[2][:, 0:nt]), reads=['RS2'], writes=['RS2'])
                for c in range(4):
                    ca('pool', lambda e, c=c: e.tensor_tensor(out=BE[:, c, 0:nt], in0=BE[:, c, 0:nt], in1=RS[1][:, 0:nt], op=ALU.subtract),
                       reads=['BE%d' % c, 'RS1'], writes=['BE%d' % c])
                    ca('dve', lambda e, c=c: e.tensor_tensor(out=BE[:, c, 0:nt], in0=BE[:, c, 0:nt], in1=RS[2][:, 0:nt], op=ALU.mult),
                       reads=['BE%d' % c, 'RS2'], writes=['BE%d' % c])
                    ca('act', lambda e, c=c: e.activation(out=BE[:, c, 0:nt], in_=BE[:, c, 0:nt], func=AF.Identity,
                                                          scale=vs(c, 40), bias=vs(c, 41)), reads=['BE%d' % c, 'V5T'], writes=['BE%d' % c])
                    ca('act', lambda e, c=c: e.activation(out=tC[:, 0:nt], in_=BE[:, c, 0:nt], func=AF.Tanh, scale=0.5),
                       reads=['BE%d' % c], writes=['tC'])
                    ca('dve', lambda e, c=c: e.scalar_tensor_tensor(
                        out=BE[:, c, 0:nt], in0=tC[:, 0:nt], scalar=1.0, in1=BE[:, c, 0:nt], op0=ALU.add, op1=ALU.mult),
                        reads=['tC', 'BE%d' % c], writes=['BE%d' % c])
                    ca('act', lambda e, c=c: e.activation(out=cb[c % 2][:, 0:nt], in_=BE[:, c, 0:nt], func=AF.Square),
                       reads=['BE%d' % c], writes=['cb%d' % (c % 2)])
                    ca('pe', lambda e, c=c: e.matmul(PS[7][:, 0:nt], lhsT=ones[:], rhs=cb[c % 2][:, 0:nt],
                                                     start=(c == 0), stop=(c == 3)), reads=['ones', 'cb%d' % (c % 2)], writes=['ps7'])
                ca('act', lambda e: e.activation(out=RS[1][:, 0:nt], in_=PS[7][:, 0:nt], func=AF.Sqrt, scale=1.0 / DL,
                                                 bias=epsc[:, 1:2]), reads=['ps7', 'epsc', 'RS1'], writes=['RS1'])
                ca('dve', lambda e: e.reciprocal(out=RS[1][:, 0:nt], in_=RS[1][:, 0:nt]), reads=['RS1'], writes=['RS1'])
                for c in range(4):
                    ca('dve', lambda e, c=c: e.scalar_tensor_tensor(
                        out=MT[:, 4 + c, 0:nt], in0=BE[:, c, 0:nt], scalar=vs(c, 43), in1=RS[1][:, 0:nt], op0=ALU.mult, op1=ALU.mult),
                        reads=['BE%d' % c, 'V5T', 'RS1'], writes=['MT'])

                nl, ncq = len(Lq), len(Cq)
                il = ic = 0
                pref = 0
                while il < nl or ic < ncq:
                    if ic >= ncq or (il < nl and il * ncq <= ic * nl):
                        a, kw = Lq[il]
                        il += 1
                    else:
                        a, kw = Cq[ic]
                        ic += 1
                    S.add(*a, **kw)
                    done = il + ic
                    if tn is not None and pref == 0 and done >= (nl + ncq) // 3:
                        pref = 1
                        fe_norm(tn, src_p, src_s, xin, xnb, stat)

                for b in range(nb):
                    xs_ = b % 2
                    xe = xep[xs_]
                    for n in range(2):
                        yb_ = 6 + n
                        for k in range(8):
                            S.add('pe', lambda e, k=k, b=b, n=n, yb_=yb_: e.matmul(
                                PS[yb_][:, :], lhsT=MT[:, k, b * 128:(b + 1) * 128], rhs=Wout[:, k, n * 512:(n + 1) * 512],
                                start=(k == 0), stop=(k == 7)), reads=['MT', 'Wout'], writes=['ps%d' % yb_])
                        S.add('dve', lambda e, n=n, yb_=yb_, gs=gs: e.tensor_tensor(
                            out=tmp[0][:], in0=PS[yb_][:, :], in1=gate[gs][:, n * 512:(n + 1) * 512], op=ALU.mult),
                            reads=['ps%d' % yb_, 'gate%d' % gs], writes=['tmp0'])
                        S.add('pool', lambda e, n=n, xe=xe: e.tensor_tensor(
                            out=xe[:, n * 512:(n + 1) * 512], in0=xe[:, n * 512:(n + 1) * 512], in1=tmp[0][:], op=ALU.add),
                            reads=['tmp0', 'xep%d' % xs_], writes=['xep%d' % xs_])
                    S.add('sp', lambda e, b=b, xe=xe: e.dma_start(out=rows(t, b, dst_p, dst_s), in_=xe[:]),
                          reads=['xep%d' % xs_], writes=['dst2'], slot='st%d' % xs_)
                    if b + 2 < nb:
                        ld(b + 2)

            fe_norm(tiles[0], src_p, src_s, xin, xnb, stat)
            for ti, t in enumerate(tiles):
                mix_tile(ti, t)
            for (srcT, ncol, dst, nm) in ((CMO, 240, o_cm, 'CMO'), (LCO, 24, o_lc, 'LCO'), (LHO, 8, o_h, 'LHO')):
                nh = 2 if ncol == 240 else 1
                w = ncol // nh
                for hh in range(nh):
                    for c in range(4):
                        S.add('pe', lambda e, srcT=srcT, c=c, hh=hh, w=w: e.transpose(
                            out=PS[2][0:w, c * 128:(c + 1) * 128], in_=srcT[:, c, hh * w:(hh + 1) * w], identity=ident[:]),
                            reads=[nm, 'ident'], writes=['ps2'])
                    S.add('dve', lambda e, w=w: e.tensor_copy(out=ostg[0:w, :], in_=PS[2][0:w, :]), reads=['ps2', 'tA0'], writes=['tA0'])
                    S.add('sp', lambda e, dst=dst, hh=hh, w=w: e.dma_start(out=dst[hh * w:(hh + 1) * w, :], in_=ostg[0:w, :]),
                          reads=['tA0'], writes=['ostates'], slot='c1')
        S.barrier(reorder=False)

    if debug_phase == 0:
        for kk in range(9):
            S.add('sp', lambda e, kk=kk: e.dma_start(out=yp[kk * 8:(kk + 1) * 8, :], in_=modS[:, kk * D:(kk + 1) * D]),
                  slot='c1')
        S.add('sp', lambda e: e.dma_start(out=ys[:, 0:192], in_=AB_A[:].rearrange("p a b c -> p (a b c)")), reads=['AB'], slot='c1')
        S.add('sp', lambda e: e.dma_start(out=ys[:, 192:384], in_=AB_B[:].rearrange("p a b c -> p (a b c)")), reads=['AB'], slot='c1')
    elif debug_phase == 2:
        mixer_phase(xp, xs, yp, ys)
    elif debug_phase == 1:
        ffn_phase(1, xp, xs, yp, ys, 0, 2, wg1, wu1, wd1, False, None)
    else:
        ffn_phase(1, xp, xs, x1p, x1s, 0, 2, wg1, wu1, wd1, False, None)
        mixer_phase(x1p, x1s, x2p, x2s)
        ffn_phase(3, x2p, x2s, yp, ys, 2, 8, wg2, wu2, wd2, True, None)

    S.barrier()
    S.add('sp', lambda e: e.nop())
    S.emit(nc, es)
    es.close()
    return nc


def _prep_inputs(inp):
    f = lambda a: np.ascontiguousarray(np.asarray(a, dtype=np.float32))
    v5 = np.concatenate([
        f(inp['lru_conv_w'])[0], f(inp['lru_conv_b']), f(inp['lru_ba']), f(inp['lru_bx']), f(inp['lru_lambda']),
        f(inp['cm_dw_w'])[0], f(inp['cm_dw_b']), f(inp['cm_ln_g']), f(inp['cm_ln_b']),
        f(inp['out_g_lru']), f(inp['out_g_conv'])], axis=0)
    assert v5.shape == (44, DL)
    shared = dict(
        w_ada=f(inp['w_ada'])[0], b_ada=f(inp['b_ada']),
        g3=np.concatenate([f(inp['ffn1_g']), f(inp['mix_g']), f(inp['ffn2_g']), f(inp['final_g'])[None, :]], axis=0),
        fgain=f(inp['final_g'])[None, :],
        wg1=f(inp['ffn1_wg'])[0], wu1=f(inp['ffn1_wu'])[0], wd1=f(inp['ffn1_wd'])[0],
        wg2=f(inp['ffn2_wg'])[0], wu2=f(inp['ffn2_wu'])[0], wd2=f(inp['ffn2_wd'])[0],
        w_in=f(inp['w_in'])[0], w_out=f(inp['w_out'])[0], v5=v5,
        wa=f(inp['lru_wa'])[0].reshape(512, 64), wx=f(inp['lru_wx'])[0].reshape(512, 64),
        identd=np.eye(128, dtype=np.float32),
    )
    xp = f(inp['x_prompt']); xs = f(inp['x_sample'])
    sh = f(inp['state_lru_h'])[0]; slc = f(inp['state_lru_conv'])[0]; scm = f(inp['state_cm_conv'])[0]
    cp = f(inp['c_prompt']); cs = f(inp['c_sample'])
    maps = []
    for c in range(NCORES):
        m = dict(shared)
        m['xp'] = xp[4 * c:4 * c + 4].reshape(NPT, D)
        m['xs'] = xs[4 * c:4 * c + 4].reshape(NST, D)
        m['st_h'] = sh[4 * c:4 * c + 4]
        m['st_lc'] = slc[4 * c:4 * c + 4].reshape(12, DL)
        m['st_cm'] = scm[4 * c:4 * c + 4].reshape(120, DL)
        m['cvec'] = np.concatenate([cp[4 * c:4 * c + 4], cs[4 * c:4 * c + 4]], axis=0)
        maps.append(m)
    return maps


def kernel(**inputs):
    maps = _prep_inputs(inputs)
    nc = build_program()
    res = run_bass_kernel_spmd(nc, maps, core_ids=list(range(NCORES)))
    R = res.results
    y_p = np.concatenate([r['yp'].reshape(4, SEQ, D) for r in R], axis=0)
    y_s = np.concatenate([r['ys'].reshape(4, DSEQ, D) for r in R], axis=0)
    hp = np.concatenate([r['o_h'][0:4] for r in R], axis=0)[None]
    hs = np.concatenate([r['o_h'][4:8] for r in R], axis=0)[None]
    lcp = np.concatenate([r['o_lc'][0:12].reshape(4, 3, DL) for r in R], axis=0)[None]
    lcs = np.concatenate([r['o_lc'][12:24].reshape(4, 3, DL) for r in R], axis=0)[None]
    cmp_ = np.concatenate([r['o_cm'][0:120].reshape(4, 30, DL) for r in R], axis=0)[None]
    cms = np.concatenate([r['o_cm'][120:240].reshape(4, 30, DL) for r in R], axis=0)[None]
    return (y_p, y_s, hp, lcp, cmp_, hs, lcs, cms)
```

```python
import contextlib
import numpy as np
import concourse.bass as bass
import concourse.mybir as mybir
from concourse.bass_utils import run_bass_kernel_spmd

F32 = mybir.dt.float32
BF16 = mybir.dt.bfloat16
AF = mybir.ActivationFunctionType
ALU = mybir.AluOpType

NCORES = 8
D = 1024
DFF = 2816
NJ = DFF // 128
DL = 512
SEQ = 2048
DSEQ = 32
PSEQ_PER_CORE = 4
SSEQ_PER_CORE = 4
NPT = PSEQ_PER_CORE * SEQ
NST = SSEQ_PER_CORE * DSEQ
NSEQ = 8
EPS = 1e-6
NT = 512


class _FakeIns:
    def then_inc(self, *a, **k):
        return self


class _FakeEng:
    def __init__(self):
        self.rec = None

    def __getattr__(self, name):
        def f(*a, **kw):
            self.rec = (name, a, kw)
            return _FakeIns()
        return f


def _fsz(ap):
    sh = ap.shape
    n = 1
    for d in sh[1:]:
        n *= int(d)
    return n


def _cost(eng, rec):
    name, a, kw = rec
    out = kw.get('out', a[0] if a else None)
    try:
        n = _fsz(out) if out is not None else 1
    except Exception:
        n = 1
    if name == 'dma_start':
        try:
            nbytes = out.size() * (2 if out.dtype == BF16 else 4)
        except Exception:
            nbytes = 1 << 16
        return 2500.0 + nbytes / 150.0
    if eng == 'pe':
        if name == 'transpose':
            return 160.0
        rhs = kw.get('rhs')
        nn = _fsz(rhs)
        mul = 4.0 if rhs.dtype == F32 else 1.0
        return max(nn, 64) / 2.4 * mul + 4.0
    if eng == 'act':
        return 230.0 + 0.96 * n + (190.0 if kw.get('accum_out') is not None else 0.0)
    if eng == 'dve':
        if name == 'reciprocal':
            return 100.0 + 6.1 * n
        if name == 'tensor_tensor_scan':
            return 70.0 + 2.0 * n
        if name in ('tensor_tensor', 'scalar_tensor_tensor'):
            return 70.0 + 1.3 * n
        if name == 'memset':
            return 60.0 + 0.5 * n
        return 60.0 + 1.0 * n
    if eng == 'pool':
        if name == 'tensor_tensor':
            return 100.0 + 2.4 * n
        if name == 'memset':
            return 100.0 + 1.0 * n
        return 100.0 + 3.5 * n
    return 30.0


def _tbl(rec):
    name, a, kw = rec
    if name != 'activation':
        return None
    f = kw.get('func')
    if f in (AF.Sqrt,):
        return 'sqrt'
    if f in (AF.Tanh, AF.Exp):
        return 'tanhexp'
    if f in (AF.Silu,):
        return 'silu'
    if f in (AF.Ln,):
        return 'ln'
    return None


class Sched:
    ENGS = ['pe', 'act', 'dve', 'pool', 'sp']
    WINDOW = 700

    def __init__(self):
        self.ops = []
        self.lastw = {}
        self.readers = {}
        self.region = 0
        self.reorder = {0: True}

    def barrier(self, reorder=True):
        self.region += 1
        self.reorder[self.region] = reorder

    def add(self, eng, fn, reads=(), writes=(), slot=None):
        idx = len(self.ops)
        deps = {}
        for r in reads:
            w = self.lastw.get(r)
            if w is not None:
                deps[w] = True
        for r in writes:
            w = self.lastw.get(r)
            if w is not None:
                deps.setdefault(w, False)
            for rd in self.readers.get(r, ()):
                deps.setdefault(rd, False)
        keep = []
        alld = []
        for d, raw in deps.items():
            o = self.ops[d]
            if o['region'] != self.region:
                continue
            alld.append(d)
            if o['slot'] is None and o['eng'] == eng and slot is None and eng == 'pe':
                continue
            keep.append(d)
        fk = _FakeEng()
        fn(fk)
        rec = fk.rec
        op = dict(eng=eng, fn=fn, deps=keep, alld=alld, slot=slot, val=None, used=False, region=self.region,
                  cost=_cost(eng, rec), tbl=_tbl(rec), idx=idx)
        for d in keep:
            self.ops[d]['used'] = True
        self.ops.append(op)
        for r in reads:
            self.readers.setdefault(r, []).append(idx)
        for r in writes:
            self.lastw[r] = idx
            self.readers[r] = []
        return idx

    def schedule(self):
        ops = self.ops
        nreg = self.region + 1
        byreg = [[] for _ in range(nreg)]
        for o in ops:
            byreg[o['region']].append(o['idx'])
        order = {e: [] for e in self.ENGS}
        regend = []
        tfree = {e: 0.0 for e in self.ENGS}
        fin = {}
        lasttbl = [None]
        for r in range(nreg):
            ids = byreg[r]
            t0 = max(tfree.values()) if r > 0 else 0.0
            for e in self.ENGS:
                tfree[e] = max(tfree[e], t0)
            ndep = {i: len(ops[i]['alld']) for i in ids}
            users = {i: [] for i in ids}
            for i in ids:
                for d in ops[i]['alld']:
                    users[d].append(i)
            ready = {e: [] for e in self.ENGS}
            pos = 0
            done = set()
            for i in ids:
                if ndep[i] == 0:
                    ready[ops[i]['eng']].append(i)
            nsched = 0
            ntot = len(ids)
            strict = ['sp'] if self.reorder.get(r, True) else list(self.ENGS)
            elist = {e: [i for i in ids if ops[i]['eng'] == e] for e in strict}
            eptr = {e: 0 for e in strict}
            idpos = {i: k for k, i in enumerate(ids)}
            while nsched < ntot:
                while pos < ntot and ids[pos] in done:
                    pos += 1
                lim = ids[min(pos + self.WINDOW, ntot - 1)] if pos < ntot else 0
                best = None
                for e in self.ENGS:
                    lst = ready[e]
                    if not lst:
                        continue
                    cands = lst
                    if e in eptr:
                        el = elist[e]
                        while eptr[e] < len(el) and el[eptr[e]] in done:
                            eptr[e] += 1
                        if eptr[e] >= len(el) or el[eptr[e]] not in lst:
                            continue
                        cands = [el[eptr[e]]]
                    for i in cands:
                        if i > lim:
                            continue
                        o = ops[i]
                        st = tfree[e]
                        for d in o['alld']:
                            if fin[d] > st:
                                st = fin[d]
                        pen = 0.0
                        if e == 'act' and o['tbl'] is not None and lasttbl[0] is not None and o['tbl'] != lasttbl[0]:
                            pen = 1300.0
                        key = (st + pen, i)
                        if best is None or key < best[0]:
                            best = (key, i, st, pen)
                if best is None:
                    allr = [i for e in self.ENGS for i in ready[e]]
                    i = min(allr)
                    o = ops[i]
                    st = tfree[o['eng']]
                    for d in o['alld']:
                        st = max(st, fin[d])
                    best = ((st, i), i, st, 0.0)
                _, i, st, pen = best
                o = ops[i]
                e = o['eng']
                ready[e].remove(i)
                if o['slot'] is not None:
                    tfree[e] = st + 60.0
                    fin[i] = st + o['cost']
                else:
                    if e == 'act' and o['tbl'] is not None:
                        lasttbl[0] = o['tbl']
                    tfree[e] = st + pen + o['cost']
                    fin[i] = tfree[e] + (200.0 if e == 'pe' else 60.0)
                order[e].append(i)
                done.add(i)
                nsched += 1
                for u in users[i]:
                    ndep[u] -= 1
                    if ndep[u] == 0:
                        ready[ops[u]['eng']].append(u)
            regend.append({e: len(order[e]) for e in self.ENGS})
        self.order = order
        self.regend = regend
        self.makespan = max(tfree.values())

    def emit(self, nc, stack):
        self.schedule()
        ops = self.ops
        engs = self.ENGS
        order = self.order
        slots = sorted({o['slot'] for o in ops if o['slot'] is not None})
        esem = {e: stack.enter_context(nc.semaphore("s_" + e)) for e in engs}
        ssem = {s: stack.enter_context(nc.semaphore("d_" + s)) for s in slots}
        nreg = len(self.regend)
        for r in range(nreg):
            for e in engs:
                lo = self.regend[r - 1][e] if r > 0 else 0
                hi = self.regend[r][e]
                for k in range(hi - 1, lo - 1, -1):
                    o = ops[order[e][k]]
                    if o['slot'] is None:
                        o['used'] = True
                        break
        cnt = {e: 0 for e in engs}
        scnt = {s: 0 for s in slots}
        ecnt_at = [dict() for _ in range(nreg)]
        scnt_at = [dict() for _ in range(nreg)]
        for e in engs:
            r = 0
            for k, i in enumerate(order[e]):
                o = ops[i]
                if o['slot'] is None:
                    if o['used']:
                        cnt[e] += 1
                        o['val'] = cnt[e]
        for e in engs:
            for i in order[e]:
                o = ops[i]
                if o['slot'] is not None:
                    scnt[o['slot']] += 16
                    o['val'] = scnt[o['slot']]
        for r in range(nreg):
            for e in engs:
                hi = self.regend[r][e]
                c = 0
                sc = {}
                for i in order[e][:hi]:
                    o = ops[i]
                    if o['slot'] is None:
                        if o['used']:
                            c = o['val']
                    else:
                        sc[o['slot']] = o['val']
                ecnt_at[r][e] = c
                for s_, v in sc.items():
                    scnt_at[r][s_] = max(scnt_at[r].get(s_, 0), v)
        block = stack.enter_context(nc.Block())

        def run(engname, e):
            known = {}

            def wait(key, sem, v):
                if v <= 0 or known.get(key, 0) >= v:
                    return
                e.wait_ge(sem, v)
                known[key] = v
            r = 0
            for k, i in enumerate(order[engname]):
                o = ops[i]
                while o['region'] > r:
                    for f in engs:
                        if f != engname:
                            wait(f, esem[f], ecnt_at[r][f])
                    for s_, v in scnt_at[r].items():
                        wait('d_' + s_, ssem[s_], v)
                    r += 1
                for d in o['deps']:
                    p = ops[d]
                    if p['slot'] is not None:
                        wait('d_' + p['slot'], ssem[p['slot']], p['val'])
                    else:
                        wait(p['eng'], esem[p['eng']], p['val'])
                ins = o['fn'](e)
                if o['slot'] is not None:
                    ins.then_inc(ssem[o['slot']], 16)
                elif o['used']:
                    ins.then_inc(esem[engname], 1)

        @block.tensor
        def _(e):
            run('pe', e)

        @block.scalar
        def _(e):
            run('act', e)

        @block.vector
        def _(e):
            run('dve', e)

        @block.gpsimd
        def _(e):
            run('pool', e)

        @block.sync
        def _(e):
            run('sp', e)


def build_program(debug_phase=99, tile_sel=None):
    nc = bass.Bass("TRN2", target_bir_lowering=False)
    S = Sched()
    es = contextlib.ExitStack()

    def din(name, shape):
        return nc.dram_tensor(name, list(shape), F32, kind="ExternalInput").ap()

    def dout(name, shape):
        return nc.dram_tensor(name, list(shape), F32, kind="ExternalOutput").ap()

    def dint(name, shape):
        return nc.dram_tensor(name, list(shape), F32, kind="Internal").ap()

    xp = din("xp", [NPT, D])
    xs = din("xs", [NST, D])
    st_h = din("st_h", [SSEQ_PER_CORE, DL])
    st_lc = din("st_lc", [SSEQ_PER_CORE * 3, DL])
    st_cm = din("st_cm", [SSEQ_PER_CORE * 30, DL])
    cvec = din("cvec", [NSEQ, D])
    w_ada = din("w_ada", [D, 9 * D])
    b_ada = din("b_ada", [1, 9 * D])
    g3 = din("g3", [4, D])
    fgain = din("fgain", [1, D])
    wg1 = din("wg1", [D, DFF]); wu1 = din("wu1", [D, DFF]); wd1 = din("wd1", [DFF, D])
    wg2 = din("wg2", [D, DFF]); wu2 = din("wu2", [D, DFF]); wd2 = din("wd2", [DFF, D])
    w_in = din("w_in", [D, 2 * D])
    w_out = din("w_out", [D, D])
    v5 = din("v5", [44, DL])
    wa = din("wa", [8 * 64, 64])
    wx = din("wx", [8 * 64, 64])

    yp = dout("yp", [NPT, D])
    ys = dout("ys", [NST, D])
    o_h = dout("o_h", [NSEQ, DL])
    o_lc = dout("o_lc", [NSEQ * 3, DL])
    o_cm = dout("o_cm", [NSEQ * 30, DL])

    x1p = dint("x1p", [NPT, D]); x1s = dint("x1s", [NST, D])
    x2p = dint("x2p", [NPT, D]); x2s = dint("x2s", [NST, D])
    modS = dint("modS", [NSEQ, 9 * D])

    tiles = []
    for s in range(PSEQ_PER_CORE):
        for j in range(SEQ // NT):
            tiles.append(dict(kind='p', nt=NT, nb=NT // 128, segs=[(s, 0, NT)], row0=s * SEQ + j * NT,
                              first=(j == 0), last=(j == SEQ // NT - 1)))
    tiles.append(dict(kind='s', nt=NST, nb=1, segs=[(4 + q, q * DSEQ, DSEQ) for q in range(4)], row0=0,
                      first=True, last=True))

    if tile_sel is not None:
        tiles = [tiles[i] for i in tile_sel]

    def rows(t, b, tp, ts):
        base = tp if t['kind'] == 'p' else ts
        r0 = t['row0'] + b * 128
        return base[r0:r0 + 128, :]

    used = {}

    def alloc(stack, name, shape, dt):
        nb_ = int(np.prod(shape[1:])) * (2 if dt == BF16 else 4)
        used[name] = nb_
        return stack.enter_context(nc.sbuf_tensor(name, list(shape), dt))
    build_program.used = used

    PS = [es.enter_context(nc.psum_tensor("ps%d" % i, [128, 512], F32)) for i in range(8)]

    PSB = [PS[i][:].bitcast(BF16) for i in range(2)]
    ident = alloc(es, "ident", [128, 128], F32)
    identb = alloc(es, "identb", [128, 128], BF16)
    AB_A = alloc(es, "AB_A", [128, 3, NSEQ, 8], F32)
    AB_B = alloc(es, "AB_B", [128, 3, NSEQ, 8], F32)
    small = alloc(es, "small", [128, 64], F32)

    identd = din("identd", [128, 128])
    S.add('sp', lambda e: e.dma_start(out=ident[:], in_=identd[:, :]), writes=['ident'], slot='ci')
    S.add('dve', lambda e: e.tensor_copy(out=identb[:], in_=ident[:]), reads=['ident'], writes=['identb'])

    with contextlib.ExitStack() as p0:
        c_sb = alloc(p0, "c_sb", [NSEQ, D], F32)
        cs_sb = alloc(p0, "cs_sb", [NSEQ, D], F32)
        cT = alloc(p0, "cT", [128, 8, NSEQ], BF16)
        modsb = alloc(p0, "modsb", [NSEQ, 9 * D], F32)
        bada = alloc(p0, "bada", [NSEQ, 9 * D], F32)
        g3sb = alloc(p0, "g3sb", [4, D], F32)
        g3T = alloc(p0, "g3T", [128, 8, 4], F32)
        modT = alloc(p0, "modT", [128, 48, NSEQ], F32)
        wslab = [alloc(p0, "wslab%d" % i, [128, 8, 512], BF16) for i in range(3)]

        S.add('sp', lambda e: e.dma_start(out=c_sb[:], in_=cvec[:, :]), writes=['c_sb'], slot='cA')
        S.add('sp', lambda e: e.dma_start(out=bada[:], in_=b_ada[0, :].partition_broadcast(NSEQ)),
              writes=['bada'], slot='cB')
        S.add('sp', lambda e: e.dma_start(out=g3sb[:], in_=g3[:, :]), writes=['g3sb'], slot='cC')
        S.add('act', lambda e: e.activation(out=cs_sb[:], in_=c_sb[:], func=AF.Tanh, scale=0.5),
              reads=['c_sb'], writes=['cs_sb'])
        S.add('dve', lambda e: e.scalar_tensor_tensor(out=cs_sb[:], in0=cs_sb[:], scalar=1.0, in1=c_sb[:],
                                                      op0=ALU.add, op1=ALU.mult),
              reads=['cs_sb', 'c_sb'], writes=['cs_sb'])
        S.add('dve', lambda e: e.tensor_scalar(out=cs_sb[:], in0=cs_sb[:], scalar1=0.5, scalar2=None, op0=ALU.mult),
              reads=['cs_sb'], writes=['cs_sb'])
        for k in range(8):
            S.add('pe', lambda e, k=k: e.transpose(out=PS[0][:, k * 8:(k + 1) * 8], in_=cs_sb[:, k * 128:(k + 1) * 128],
                                                   identity=ident[0:NSEQ, 0:NSEQ]),
                  reads=['cs_sb', 'ident'], writes=['ps0'])
        S.add('dve', lambda e: e.tensor_copy(out=cT[:].rearrange("p k s -> p (k s)"), in_=PS[0][:, 0:64]),
              reads=['ps0'], writes=['cT'])
        for k in range(8):
            S.add('pe', lambda e, k=k: e.transpose(out=PS[1][:, k * 4:(k + 1) * 4], in_=g3sb[:, k * 128:(k + 1) * 128],
                                                   identity=ident[0:4, 0:4]),
                  reads=['g3sb', 'ident'], writes=['ps1'])
        S.add('dve', lambda e: e.tensor_copy(out=g3T[:].rearrange("p k s -> p (k s)"), in_=PS[1][:, 0:32]),
              reads=['ps1'], writes=['g3T'])
        for sl in range(18):
            wb = wslab[sl % 3]
            wn = "wslab%d" % (sl % 3)
            S.add('pool', lambda e, sl=sl, wb=wb: e.dma_start(
                out=wb[:], in_=w_ada[:, sl * 512:(sl + 1) * 512].rearrange("(k p) n -> p k n", p=128)),
                writes=[wn], slot='wsl%d' % (sl % 3))
            bank = 2 + (sl % 2)
            for k in range(8):
                S.add('pe', lambda e, k=k, wb=wb, bank=bank: e.matmul(PS[bank][0:NSEQ, :], lhsT=cT[:, k, :], rhs=wb[:, k, :],
                                                                      start=(k == 0), stop=(k == 7)),
                      reads=['cT', wn], writes=['ps%d' % bank])
            S.add('dve', lambda e, sl=sl, bank=bank: e.tensor_tensor(
                out=modsb[:, sl * 512:(sl + 1) * 512], in0=PS[bank][0:NSEQ, :], in1=bada[:, sl * 512:(sl + 1) * 512],
                op=ALU.add), reads=['ps%d' % bank, 'bada'], writes=['modsb'])
        for kk in (2, 8):
            S.add('dve', lambda e, kk=kk: e.tensor_scalar(out=modsb[:, kk * D:(kk + 1) * D], in0=modsb[:, kk * D:(kk + 1) * D],
                                                          scalar1=0.5, scalar2=None, op0=ALU.mult),
                  reads=['modsb'], writes=['modsb'])
        S.add('sp', lambda e: e.dma_start(out=modS[:, :], in_=modsb[:]), reads=['modsb'], writes=['modS'], slot='c1')
        kinds = [0, 1, 3, 4, 6, 7]
        for qi, kk in enumerate(kinds):
            for k in range(8):
                col = (qi * 8 + k) * NSEQ
                bank = 4 + (qi * 8 + k) // 32
                S.add('pe', lambda e, kk=kk, k=k, col=col: e.transpose(
                    out=PS[4][:, col:col + NSEQ], in_=modsb[:, kk * D + k * 128: kk * D + (k + 1) * 128],
                    identity=ident[0:NSEQ, 0:NSEQ]), reads=['modsb', 'ident'], writes=['ps4'])
        S.add('dve', lambda e: e.tensor_copy(out=modT[:].rearrange("p a s -> p (a s)"), in_=PS[4][:, 0:48 * NSEQ]),
              reads=['ps4'], writes=['modT'])
        for k3 in range(3):
            for s in range(NSEQ):
                S.add('dve', lambda e, k3=k3, s=s: e.scalar_tensor_tensor(
                    out=AB_A[:, k3, s, :], in0=modT[:, (2 * k3 + 1) * 8:(2 * k3 + 2) * 8, s], scalar=1.0,
                    in1=g3T[:, :, k3], op0=ALU.add, op1=ALU.mult), reads=['modT', 'g3T'], writes=['AB'])
                S.add('dve', lambda e, k3=k3, s=s: e.tensor_copy(
                    out=AB_B[:, k3, s, :], in_=modT[:, (2 * k3) * 8:(2 * k3 + 1) * 8, s]), reads=['modT'], writes=['AB'])
    S.barrier(reorder=False)

    def fe_load(t, src_p, src_s, xin, b):
        slot = b % len(xin)
        xb = xin[slot]
        xn = 'xin%d' % slot
        S.add('sp', lambda e, xb=xb, b=b: e.dma_start(out=xb[:], in_=rows(t, b, src_p, src_s)),
              writes=[xn], slot=xn)

    def fe_normb(t, xin, xnb, stat, b):
        slot = b % len(xin)
        xb = xin[slot]
        xn = 'xin%d' % slot
        xq = xnb[b]
        xqn = 'xnb%d' % b
        S.add('act', lambda e, xb=xb, b=b, xq=xq: e.activation(out=xq[:], in_=xb[:], func=AF.Square,
                                                               accum_out=stat[:, b:b + 1]),
              reads=[xn], writes=[xqn, 'stat%d' % b])
        S.add('act', lambda e, b=b: e.activation(out=stat[:, 4 + b:5 + b], in_=stat[:, b:b + 1], func=AF.Sqrt,
                                                 scale=1.0 / D, bias=epsc[:, 0:1]),
              reads=['stat%d' % b, 'epsc'], writes=['statr%d' % b])
        S.add('dve', lambda e, b=b: e.reciprocal(out=stat[:, 8 + b:9 + b], in_=stat[:, 4 + b:5 + b]),
              reads=['statr%d' % b], writes=['stati%d' % b])
        S.add('dve', lambda e, xb=xb, b=b, xq=xq: e.tensor_scalar(out=xq[:], in0=xb[:], scalar1=stat[:, 8 + b:9 + b],
                                                                  scalar2=None, op0=ALU.mult),
              reads=[xn, 'stati%d' % b, xqn], writes=[xqn])

    def fe_norm(t, src_p, src_s, xin, xnb, stat):
        nb = t['nb']
        ns = len(xin)
        for b in range(min(ns, nb)):
            fe_load(t, src_p, src_s, xin, b)
        for b in range(nb):
            fe_normb(t, xin, xnb, stat, b)
            if b + ns < nb:
                fe_load(t, src_p, src_s, xin, b + ns)

    def fe_tr(t, k3, hT, xnb, banks=(0, 1)):
        for b in range(t['nb']):
            xq = xnb[b]
            xqn = 'xnb%d' % b
            bank = banks[b % len(banks)]
            for k in range(8):
                S.add('pe', lambda e, xq=xq, k=k, bank=bank: e.transpose(
                    out=PSB[bank][:, k * 128:(k + 1) * 128], in_=xq[:, k * 128:(k + 1) * 128], identity=identb[:]),
                    reads=[xqn, 'identb'], writes=['ps%d' % bank])
            for k in range(8):
                for (s, c0, L) in t['segs']:
                    lo = max(c0, b * 128)
                    hi = min(c0 + L, (b + 1) * 128)
                    if lo >= hi:
                        continue
                    pl = k * 128 + (lo - b * 128)
                    if k % 2 == 0:
                        S.add('act', lambda e, k=k, s=s, lo=lo, hi=hi, pl=pl, bank=bank: e.activation(
                            out=hT[:, k, lo:hi], in_=PSB[bank][:, pl:pl + hi - lo], func=AF.Identity,
                            scale=AB_A[:, k3, s, k:k + 1], bias=AB_B[:, k3, s, k:k + 1]),
                            reads=['ps%d' % bank, 'AB'], writes=['hT'])
                    else:
                        S.add('dve', lambda e, k=k, s=s, lo=lo, hi=hi, pl=pl, bank=bank: e.tensor_scalar(
                            out=hT[:, k, lo:hi], in0=PSB[bank][:, pl:pl + hi - lo],
                            scalar1=AB_A[:, k3, s, k:k + 1], scalar2=AB_B[:, k3, s, k:k + 1],
                            op0=ALU.mult, op1=ALU.add),
                            reads=['ps%d' % bank, 'AB'], writes=['hT'])

    def load_gate(t, k9, gate, gslot):
        gn = 'gate%d' % gslot
        for (s, c0, L) in t['segs']:
            if t['kind'] == 'p':
                S.add('sp', lambda e, s=s: e.dma_start(out=gate[gslot][:],
                                                       in_=modS[s, k9 * D:(k9 + 1) * D].partition_broadcast(128)),
                      reads=['modS'], writes=[gn], slot=gn)
            else:
                S.add('sp', lambda e, s=s, c0=c0, L=L: e.dma_start(
                    out=gate[gslot][c0:c0 + L, :], in_=modS[s, k9 * D:(k9 + 1) * D].partition_broadcast(L)),
                    reads=['modS'], writes=[gn], slot=gn)

    epsc = alloc(es, "epsc", [128, 4], F32)
    S.add('dve', lambda e: e.memset(epsc[:, 0:1], EPS), writes=['epsc'])
    S.add('dve', lambda e: e.memset(epsc[:, 1:2], 4.0 * EPS), writes=['epsc'])
    S.add('dve', lambda e: e.memset(epsc[:, 2:3], 1.0), writes=['epsc'])
    S.add('dve', lambda e: e.memset(epsc[:, 3:4], 0.25), writes=['epsc'])

    def ffn_phase(pi, src_p, src_s, dst_p, dst_s, k3, k9, wg, wu, wd, final, fence):
        with contextlib.ExitStack() as ph:
            Wg = alloc(ph, "Wg%d" % pi, [128, 8, DFF], BF16)
            Wu = alloc(ph, "Wu%d" % pi, [128, 8, DFF], BF16)
            Wd = alloc(ph, "Wd%d" % pi, [128, NJ, D], BF16)
            hT = alloc(ph, "hT%d" % pi, [128, 8, NT], BF16)
            aT = alloc(ph, "aT%d" % pi, [128, NJ, NT], BF16)
            xin = [alloc(ph, "xin%d_%d" % (pi, i), [128, D], F32) for i in range(2)]
            xep = [alloc(ph, "xep%d_%d" % (pi, i), [128, D], F32) for i in range(2)]
            xnb = [alloc(ph, "xnb%d_%d" % (pi, i), [128, D], BF16) for i in range(4)]
            sg = [alloc(ph, "sg%d_%d" % (pi, i), [128, NT], BF16) for i in range(2)]
            tmp = [alloc(ph, "tmp%d_%d" % (pi, i), [128, 512], F32) for i in range(2)]
            gate = [alloc(ph, "gate%d_%d" % (pi, i), [128, D], F32) for i in range(2)]
            stat = alloc(ph, "stat%d" % pi, [128, 16], F32)
            fg = alloc(ph, "fg%d" % pi, [128, D], F32) if final else None
            allres = ['Wg', 'Wu', 'Wd', 'hT', 'aT', 'xin0', 'xin1', 'xep0', 'xep1', 'xep2', 'sq', 'sg0', 'sg1',
                      'tmp0', 'tmp1', 'gate0', 'gate1', 'fg'] + ['stat%d' % b for b in range(4)] + \
                     ['statr%d' % b for b in range(4)] + ['stati%d' % b for b in range(4)] + ['fstat']
            NG = (NJ + 3) // 4
            wtk = [0]

            def tok():
                wtk[0] += 1
                return ['wtok%d' % (wtk[0] % 3)]
            for g in range(NG):
                c0_, c1_ = g * 512, min(DFF, (g + 1) * 512)
                S.add('pool', lambda e, c0_=c0_, c1_=c1_: e.dma_start(
                    out=Wg[:, :, c0_:c1_], in_=wg[:, c0_:c1_].rearrange("(k p) n -> p k n", p=128)),
                    writes=['Wg%d' % g] + tok(), slot='wG%d' % g)
                S.add('pool', lambda e, c0_=c0_, c1_=c1_: e.dma_start(
                    out=Wu[:, :, c0_:c1_], in_=wu[:, c0_:c1_].rearrange("(k p) n -> p k n", p=128)),
                    writes=['Wu%d' % g] + tok(), slot='wU%d' % g)
            for j in range(NJ):
                S.add('pool', lambda e, j=j: e.dma_start(out=Wd[:, j, :], in_=wd[j * 128:(j + 1) * 128, :]),
                      writes=['Wd%d' % j] + tok(), slot='wD%d' % j)
            if final:
                S.add('sp', lambda e: e.dma_start(out=fg[:], in_=fgain[0, :].partition_broadcast(128)),
                      writes=['fg'], slot='cD')
            gslot_of_tile = []
            new_gate = []
            gcount = 0
            prev_key = None
            for ti, t in enumerate(tiles):
                key = (t['kind'], t['segs'][0][0])
                if key != prev_key:
                    gcount += 1
                    new_gate.append(True)
                else:
                    new_gate.append(False)
                prev_key = key
                gslot_of_tile.append((gcount - 1) % 2)

            def gu(ti, t):
                nt = t['nt']
                if new_gate[ti]:
                    load_gate(t, k9, gate, gslot_of_tile[ti])
                tn = tiles[ti + 1] if ti + 1 < len(tiles) else None
                for j in range(NJ):
                    if tn is not None:
                        if j == 0:
                            for b in range(min(2, tn['nb'])):
                                fe_load(tn, src_p, src_s, xin, b)
                        if j == 4:
                            for b in range(min(2, tn['nb'])):
                                fe_normb(tn, xin, xnb, stat, b)
                                if b + 2 < tn['nb']:
                                    fe_load(tn, src_p, src_s, xin, b + 2)
                        if j == 8:
                            for b in range(2, tn['nb']):
                                fe_normb(tn, xin, xnb, stat, b)
                    gb = 2 + 2 * (j % 2)
                    ub = gb + 1
                    for k in range(8):
                        S.add('pe', lambda e, j=j, k=k, gb=gb, nt=nt: e.matmul(
                            PS[gb][:, 0:nt], lhsT=Wg[:, k, j * 128:(j + 1) * 128], rhs=hT[:, k, 0:nt],
                            start=(k == 0), stop=(k == 7)), reads=['Wg%d' % (j // 4), 'hT'], writes=['ps%d' % gb])
                    for k in range(8):
                        S.add('pe', lambda e, j=j, k=k, ub=ub, nt=nt: e.matmul(
                            PS[ub][:, 0:nt], lhsT=Wu[:, k, j * 128:(j + 1) * 128], rhs=hT[:, k, 0:nt],
                            start=(k == 0), stop=(k == 7)), reads=['Wu%d' % (j // 4), 'hT'], writes=['ps%d' % ub])
                    sgi = j % 2
                    S.add('act', lambda e, gb=gb, sgi=sgi, nt=nt: e.activation(out=sg[sgi][:, 0:nt], in_=PS[gb][:, 0:nt],
                                                                               func=AF.Silu),
                          reads=['ps%d' % gb], writes=['sg%d' % sgi])
                    S.add('dve', lambda e, j=j, ub=ub, sgi=sgi, nt=nt: e.tensor_tensor(
                        out=aT[:, j, 0:nt], in0=PS[ub][:, 0:nt], in1=sg[sgi][:, 0:nt], op=ALU.mult),
                        reads=['ps%d' % ub, 'sg%d' % sgi], writes=['aT'])

            def down(ti, t):
                nb = t['nb']
                gs = gslot_of_tile[ti]

                def ld(b):
                    xs_ = b % 2
                    S.add('sp', lambda e, b=b, xs_=xs_: e.dma_start(out=xep[xs_][:], in_=rows(t, b, src_p, src_s)),
                          writes=['xep%d' % xs_], slot='xep%d' % xs_)
                for b in range(min(nb, 2)):
                    ld(b)
                for b in range(nb):
                    xs_ = b % 2
                    xe = xep[xs_]
                    for n in range(2):
                        yb = 6 + n
                        for j in range(NJ):
                            S.add('pe', lambda e, j=j, b=b, n=n, yb=yb: e.matmul(
                                PS[yb][:, :], lhsT=aT[:, j, b * 128:(b + 1) * 128], rhs=Wd[:, j, n * 512:(n + 1) * 512],
                                start=(j == 0), stop=(j == NJ - 1)), reads=['aT', 'Wd%d' % j], writes=['ps%d' % yb])
                        S.add('dve', lambda e, n=n, yb=yb, gs=gs: e.tensor_tensor(
                            out=tmp[n][:], in0=PS[yb][:, :], in1=gate[gs][:, n * 512:(n + 1) * 512], op=ALU.mult),
                            reads=['ps%d' % yb, 'gate%d' % gs], writes=['tmp%d' % n])
                        S.add('pool', lambda e, n=n, xe=xe: e.tensor_tensor(
                            out=xe[:, n * 512:(n + 1) * 512], in0=xe[:, n * 512:(n + 1) * 512], in1=tmp[n][:], op=ALU.add),
                            reads=['tmp%d' % n, 'xep%d' % xs_], writes=['xep%d' % xs_])
                    if final:
                        S.add('act', lambda e, xe=xe, b=b: e.activation(out=tmp[0][:], in_=xe[:, 0:512], func=AF.Square,
                                                                        accum_out=stat[:, 12:13]),
                              reads=['xep%d' % xs_], writes=['tmp0', 'fstat'])
                        S.add('act', lambda e, xe=xe, b=b: e.activation(out=tmp[1][:], in_=xe[:, 512:1024], func=AF.Square,
                                                                        accum_out=stat[:, 15:16]),
                              reads=['xep%d' % xs_], writes=['tmp1', 'fstat'])
                        S.add('dve', lambda e: e.tensor_tensor(out=stat[:, 12:13], in0=stat[:, 12:13], in1=stat[:, 15:16], op=ALU.add),
                              reads=['fstat'], writes=['fstat'])
                        S.add('act', lambda e: e.activation(out=stat[:, 13:14], in_=stat[:, 12:13], func=AF.Sqrt,
                                                            scale=1.0 / D, bias=epsc[:, 0:1]),
                              reads=['fstat', 'epsc'], writes=['fstat'])
                        S.add('dve', lambda e: e.reciprocal(out=stat[:, 14:15], in_=stat[:, 13:14]),
                              reads=['fstat'], writes=['fstat'])
                        S.add('dve', lambda e, xe=xe: e.scalar_tensor_tensor(
                            out=xe[:], in0=xe[:], scalar=stat[:, 14:15], in1=fg[:], op0=ALU.mult, op1=ALU.mult),
                            reads=['xep%d' % xs_, 'fstat', 'fg'], writes=['xep%d' % xs_])
                    S.add('sp', lambda e, b=b, xe=xe: e.dma_start(out=rows(t, b, dst_p, dst_s), in_=xe[:]),
                          reads=['xep%d' % xs_], writes=['dst%d' % pi], slot='st%d' % xs_)
                    if b + 2 < nb:
                        ld(b + 2)

            fe_norm(tiles[0], src_p, src_s, xin, xnb, stat)
            fe_tr(tiles[0], k3, hT, xnb)
            for ti, t in enumerate(tiles):
                gu(ti, t)
                if ti + 1 < len(tiles):
                    fe_tr(tiles[ti + 1], k3, hT, xnb)
                down(ti, t)
        S.barrier()

    def mixer_phase(src_p, src_s, dst_p, dst_s):
        k3, k9 = 1, 5
        with contextlib.ExitStack() as ph:
            Win = alloc(ph, "Win", [128, 8, 2 * D], BF16)
            Wout = alloc(ph, "Wout", [128, 8, D], BF16)
            D31 = alloc(ph, "D31", [128, 4, 31, 128], BF16)
            Wab = alloc(ph, "Wab", [128, 2, 4, 128], BF16)
            ones = alloc(ph, "ones", [128, 128], BF16)
            V5T = alloc(ph, "V5T", [128, 4, 44], F32)
            der = alloc(ph, "der", [128, 4, 4], F32)
            hT = alloc(ph, "hTm", [128, 8, NT], BF16)
            xin = [alloc(ph, "xinm%d" % i, [128, D], F32) for i in range(1)]
            xep = [alloc(ph, "xepm%d" % i, [128, D], F32) for i in range(2)]
            xnb = [alloc(ph, "xnbm%d" % i, [128, D], BF16) for i in range(4)]
            stat = alloc(ph, "statm", [128, 16], F32)
            tmp = [alloc(ph, "tmpm%d" % i, [128, 512], F32) for i in range(1)]
            gate = [alloc(ph, "gatem0", [128, D], F32)]
            LXW = 3 + NT
            CUW = 30 + NT
            LX = alloc(ph, "LX", [128, 4, LXW], F32)
            CU = alloc(ph, "CU", [128, 4, CUW], BF16)
            BA = alloc(ph, "BA", [128, 4, NT], F32)
            BB = alloc(ph, "BB", [128, 4, NT], F32)
            BC = alloc(ph, "BC", [128, 4, NT], F32)
            BD = alloc(ph, "BD", [128, 4, NT], F32)
            BE = alloc(ph, "BE", [128, 4, NT], F32)
            YLb = [alloc(ph, "YLb%d" % i, [128, NT], BF16) for i in range(2)]
            lsq = alloc(ph, "lsq", [128, NT], BF16)
            cb = [alloc(ph, "cb%d" % i, [128, NT], BF16) for i in range(2)]
            tA = [alloc(ph, "tA%d" % i, [128, NT], F32) for i in range(2)]
            tB = [alloc(ph, "tB%d" % i, [128, NT], F32) for i in range(2)]
            tC = alloc(ph, "tC", [128, NT], F32)
            RS = [alloc(ph, "RS%d" % i, [128, NT], F32) for i in range(3)]
            hst = alloc(ph, "hst", [128, 4, 4], F32)
            CMH = alloc(ph, "CMH", [128, 4, 120], F32)
            LCH = alloc(ph, "LCH", [128, 4, 16], F32)
            CMO = alloc(ph, "CMO", [128, 4, 240], F32)
            LCO = alloc(ph, "LCO", [128, 4, 24], F32)
            LHO = alloc(ph, "LHO", [128, 4, 8], F32)
            S1 = tA[0][0:120, :]
            S2 = tB[1][0:16, :]
            V5 = tB[0][0:44, :]
            ostg = tA[0][0:120, :]
            MT = alloc(ph, "MTm", [128, 8, NT], BF16)

            for k in range(8):
                S.add('pool', lambda e, k=k: e.dma_start(out=Win[:, k, :], in_=w_in[k * 128:(k + 1) * 128, :]),
                      writes=['Win'], slot='wA')
            for k in range(8):
                S.add('pool', lambda e, k=k: e.dma_start(out=Wout[:, k, :], in_=w_out[k * 128:(k + 1) * 128, :]),
                      writes=['Wout'], slot='wB')
            S.add('dve', lambda e: e.memset(Wab[:].rearrange("p a c m -> p (a c m)"), 0.0), writes=['Wab'])
            S.add('dve', lambda e: e.memset(ones[:], 1.0), writes=['ones'])
            for wi, wsrc in enumerate((wa, wx)):
                for c in range(4):
                    for hh in range(2):
                        S.add('pool', lambda e, wi=wi, wsrc=wsrc, c=c, hh=hh: e.dma_start(
                            out=Wab[hh * 64:(hh + 1) * 64, wi, c, hh * 64:(hh + 1) * 64],
                            in_=wsrc[(2 * c + hh) * 64:(2 * c + hh + 1) * 64, :]), reads=['Wab'], writes=['Wab'], slot='wC')
            S.add('sp', lambda e: e.dma_start(out=V5, in_=v5[:, :]), writes=['tB0'], slot='cA')
            S.add('sp', lambda e: e.dma_start(out=S1, in_=st_cm[:, :]), writes=['tA0'], slot='cB')
            S.add('sp', lambda e: e.dma_start(out=S2[0:12, :], in_=st_lc[:, :]), writes=['tB1'], slot='cC')
            S.add('sp', lambda e: e.dma_start(out=S2[12:16, :], in_=st_h[:, :]), writes=['tB1'], slot='cC')
            for c in range(4):
                S.add('pe', lambda e, c=c: e.transpose(out=PS[2][:, c * 44:(c + 1) * 44], in_=V5[:, c * 128:(c + 1) * 128],
                                                       identity=ident[0:44, 0:44]), reads=['tB0', 'ident'], writes=['ps2'])
                S.add('pe', lambda e, c=c: e.transpose(out=PS[3][:, c * 120:(c + 1) * 120], in_=S1[:, c * 128:(c + 1) * 128],
                                                       identity=ident[0:120, 0:120]), reads=['tA0', 'ident'], writes=['ps3'])
                S.add('pe', lambda e, c=c: e.transpose(out=PS[4][:, c * 16:(c + 1) * 16], in_=S2[:, c * 128:(c + 1) * 128],
                                                       identity=ident[0:16, 0:16]), reads=['tB1', 'ident'], writes=['ps4'])
            S.add('dve', lambda e: e.tensor_copy(out=V5T[:].rearrange("p c v -> p (c v)"), in_=PS[2][:, 0:176]),
                  reads=['ps2'], writes=['V5T'])
            S.add('dve', lambda e: e.tensor_copy(out=CMH[:].rearrange("p c v -> p (c v)"), in_=PS[3][:, 0:480]),
                  reads=['ps3'], writes=['CMH'])
            S.add('dve', lambda e: e.tensor_copy(out=LCH[:].rearrange("p c v -> p (c v)"), in_=PS[4][:, 0:64]),
                  reads=['ps4'], writes=['LCH'])
            S.add('dve', lambda e: e.tensor_scalar(out=der[:, :, 0], in0=V5T[:, :, 5], scalar1=0.5, scalar2=None, op0=ALU.mult),
                  reads=['V5T'], writes=['der'])
            S.add('dve', lambda e: e.tensor_scalar(out=der[:, :, 1], in0=V5T[:, :, 6], scalar1=0.5, scalar2=None, op0=ALU.mult),
                  reads=['V5T'], writes=['der'])
            S.add('act', lambda e: e.activation(out=der[:, :, 2], in_=V5T[:, :, 7], func=AF.Exp, scale=-1.0),
                  reads=['V5T', 'der'], writes=['der'])
            S.add('act', lambda e: e.activation(out=der[:, :, 2], in_=der[:, :, 2], func=AF.Ln, bias=epsc[:, 2:3]),
                  reads=['der', 'epsc'], writes=['der'])
            S.add('dve', lambda e: e.tensor_scalar(out=der[:, :, 3], in0=der[:, :, 2], scalar1=-4.0, scalar2=None, op0=ALU.mult),
                  reads=['der'], writes=['der'])
            S.add('dve', lambda e: e.tensor_scalar(out=der[:, :, 2], in0=der[:, :, 2], scalar1=-8.0, scalar2=None, op0=ALU.mult),
                  reads=['der'], writes=['der'])
            for c in range(4):
                for k in range(31):
                    S.add('pool' if k % 3 == 0 else 'dve', lambda e, c=c, k=k: e.tensor_scalar(
                        out=D31[:, c, k, :], in0=ident[:], scalar1=V5T[:, c, 8 + k:9 + k], scalar2=0.5,
                        op0=ALU.mult, op1=ALU.mult), reads=['ident', 'V5T'], writes=['D31_%d_%d' % (c, k)])
            S.add('pool', lambda e: e.memset(CMO[:].rearrange("p c v -> p (c v)"), 0.0), writes=['CMO'])
            S.add('pool', lambda e: e.memset(LCO[:].rearrange("p c v -> p (c v)"), 0.0), writes=['LCO'])
            S.add('pool', lambda e: e.memset(LHO[:].rearrange("p c v -> p (c v)"), 0.0), writes=['LHO'])

            def vs(c, i):
                return V5T[:, c, i:i + 1]

            new_gate = []
            prev_key = None
            for ti, t in enumerate(tiles):
                key = (t['kind'], t['segs'][0][0])
                new_gate.append(key != prev_key)
                prev_key = key

            def mix_tile(ti, t):
                nt = t['nt']
                nb = t['nb']
                segs = [(sq_, c0, L, j, j * (3 + L), j * (30 + L)) for j, (sq_, c0, L) in enumerate(t['segs'])]
                gs = 0
                tn = tiles[ti + 1] if ti + 1 < len(tiles) else None
                if new_gate[ti]:
                    load_gate(t, k9, gate, gs)

                def ld(b):
                    xs_ = b % 2
                    S.add('sp', lambda e, b=b, xs_=xs_: e.dma_start(out=xep[xs_][:], in_=rows(t, b, src_p, src_s)),
                          writes=['xep%d' % xs_], slot='xep%d' % xs_)
                for b in range(min(nb, 2)):
                    ld(b)
                if t['kind'] == 'p':
                    if t['first']:
                        S.add('pool', lambda e: e.memset(LX[:, :, 0:3], 0.0), writes=['LX'])
                        S.add('pool', lambda e: e.memset(CU[:, :, 0:30], 0.0), writes=['CU'])
                        S.add('pool', lambda e: e.memset(hst[:].rearrange("p c j -> p (c j)"), 0.0), writes=['hst'])
                    else:
                        S.add('pool', lambda e: e.tensor_copy(out=LX[:, :, 0:3], in_=LX[:, :, NT:NT + 3]),
                              reads=['LX'], writes=['LX'])
                        S.add('pool', lambda e: e.tensor_copy(out=CU[:, :, 0:30], in_=CU[:, :, NT:NT + 30]),
                              reads=['CU'], writes=['CU'])
                else:
                    for (sq_, c0, L, j, lxo, cuo) in segs:
                        S.add('pool', lambda e, j=j, lxo=lxo: e.tensor_copy(out=LX[:, :, lxo:lxo + 3], in_=LCH[:, :, j * 3:j * 3 + 3]),
                              reads=['LX', 'LCH'], writes=['LX'])
                        S.add('pool', lambda e, j=j, cuo=cuo: e.tensor_scalar(
                            out=CU[:, :, cuo:cuo + 30], in0=CMH[:, :, j * 30:j * 30 + 30], scalar1=2.0, scalar2=None,
                            op0=ALU.mult), reads=['CU', 'CMH'], writes=['CU'])
                        S.add('pool', lambda e, j=j: e.tensor_copy(out=hst[:, :, j], in_=LCH[:, :, 12 + j]),
                              reads=['LCH', 'hst'], writes=['hst'])
                fe_tr(t, k3, hT, xnb, banks=(0,))

                Lq = []
                Cq = []

                def la(*a, **kw):
                    Lq.append((a, kw))

                def ca(*a, **kw):
                    Cq.append((a, kw))

                def zmm(add, m, bank):
                    for k in range(8):
                        add('pe', lambda e, m=m, k=k, bank=bank: e.matmul(
                            PS[bank][:, 0:nt], lhsT=Win[:, k, m * 128:(m + 1) * 128], rhs=hT[:, k, 0:nt],
                            start=(k == 0), stop=(k == 7)), reads=['Win', 'hT'], writes=['ps%d' % bank])

                for c in range(4):
                    bank = 2 + (c % 2)
                    zmm(la, c, bank)
                    for (sq_, c0, L, j, lxo, cuo) in segs:
                        la('act', lambda e, c=c, bank=bank, c0=c0, L=L, lxo=lxo: e.activation(
                            out=LX[:, c, lxo + 3:lxo + 3 + L], in_=PS[bank][:, c0:c0 + L], func=AF.Copy),
                            reads=['ps%d' % bank, 'LX'], writes=['LX'])
                    for (sq_, c0, L, j, lxo, cuo) in segs:
                        la('dve', lambda e, c=c, c0=c0, L=L, lxo=lxo: e.tensor_scalar(
                            out=BA[:, c, c0:c0 + L], in0=LX[:, c, lxo:lxo + L], scalar1=vs(c, 0), scalar2=vs(c, 4),
                            op0=ALU.mult, op1=ALU.add), reads=['LX', 'V5T'], writes=['BA%d' % c])
                        for k in range(1, 4):
                            la('dve', lambda e, c=c, c0=c0, L=L, lxo=lxo, k=k: e.scalar_tensor_tensor(
                                out=BA[:, c, c0:c0 + L], in0=LX[:, c, lxo + k:lxo + k + L], scalar=vs(c, k),
                                in1=BA[:, c, c0:c0 + L], op0=ALU.mult, op1=ALU.add), reads=['LX', 'V5T', 'BA%d' % c], writes=['BA%d' % c])
                    yb = YLb[c % 2]
                    ybn = 'YLb%d' % (c % 2)
                    la('act', lambda e, c=c, yb=yb: e.activation(out=yb[:, 0:nt], in_=BA[:, c, 0:nt], func=AF.Copy),
                       reads=['BA%d' % c], writes=[ybn])
                    bank2 = 2 + ((c + 1) % 2)
                    zmm(la, 4 + c, bank2)
                    la('dve', lambda e, c=c, bank2=bank2: e.tensor_copy(out=BB[:, c, 0:nt], in_=PS[bank2][:, 0:nt]),
                       reads=['ps%d' % bank2], writes=['BB%d' % c])
                    ra, ix = 2 + (c % 2), 2 + ((c + 1) % 2)
                    la('pe', lambda e, c=c, yb=yb, ra=ra: e.matmul(PS[ra][:, 0:nt], lhsT=Wab[:, 0, c, :], rhs=yb[:, 0:nt],
                                                                  start=True, stop=True), reads=['Wab', ybn], writes=['ps%d' % ra])
                    la('pe', lambda e, c=c, yb=yb, ix=ix: e.matmul(PS[ix][:, 0:nt], lhsT=Wab[:, 1, c, :], rhs=yb[:, 0:nt],
                                                                  start=True, stop=True), reads=['Wab', ybn], writes=['ps%d' % ix])
                    tr = tA[c % 2]
                    trn = 'tA%d' % (c % 2)
                    ti_ = tB[c % 2]
                    tin = 'tB%d' % (c % 2)
                    la('act', lambda e, c=c, ra=ra, tr=tr: e.activation(out=tr[:, 0:nt], in_=PS[ra][:, 0:nt], func=AF.Tanh,
                                                                       scale=0.5, bias=der[:, c, 0:1]),
                       reads=['ps%d' % ra, 'der'], writes=[trn])
                    la('act', lambda e, c=c, ix=ix, ti_=ti_: e.activation(out=ti_[:, 0:nt], in_=PS[ix][:, 0:nt], func=AF.Tanh,
                                                                         scale=0.5, bias=der[:, c, 1:2]),
                       reads=['ps%d' % ix, 'der'], writes=[tin])
                    la('act', lambda e, c=c, tr=tr: e.activation(out=BD[:, c, 0:nt], in_=tr[:, 0:nt], func=AF.Exp,
                                                                 scale=der[:, c, 3:4], bias=der[:, c, 3:4]),
                       reads=[trn, 'der'], writes=['BD%d' % c])
                    la('act', lambda e, c=c, tr=tr: e.activation(out=BC[:, c, 0:nt], in_=tr[:, 0:nt], func=AF.Exp,
                                                                 scale=der[:, c, 2:3], bias=der[:, c, 2:3]),
                       reads=[trn, 'der'], writes=['BC%d' % c])
                    la('dve', lambda e, c=c, ti_=ti_: e.scalar_tensor_tensor(
                        out=BA[:, c, 0:nt], in0=ti_[:, 0:nt], scalar=1.0, in1=BA[:, c, 0:nt], op0=ALU.add, op1=ALU.mult),
                        reads=[tin, 'BA%d' % c], writes=['BA%d' % c])
                    tq = tB[c % 2]
                    tqn = 'tB%d' % (c % 2)
                    la('act', lambda e, c=c, tq=tq: e.activation(out=tq[:, 0:nt], in_=BB[:, c, 0:nt], func=AF.Square),
                       reads=['BB%d' % c], writes=[tqn])
                    la('dve', lambda e, tq=tq: e.tensor_scalar(out=tq[:, 0:nt], in0=tq[:, 0:nt], scalar1=0.044715, scalar2=1.0,
                                                               op0=ALU.mult, op1=ALU.add), reads=[tqn], writes=[tqn])
                    la('dve', lambda e, c=c, tq=tq: e.tensor_tensor(out=tq[:, 0:nt], in0=tq[:, 0:nt], in1=BB[:, c, 0:nt], op=ALU.mult),
                       reads=[tqn, 'BB%d' % c], writes=[tqn])
                    la('act', lambda e, tq=tq: e.activation(out=tq[:, 0:nt], in_=tq[:, 0:nt], func=AF.Tanh, scale=0.7978845608028654),
                       reads=[tqn], writes=[tqn])
                    la('dve', lambda e, c=c, tq=tq: e.scalar_tensor_tensor(
                        out=BB[:, c, 0:nt], in0=tq[:, 0:nt], scalar=1.0, in1=BB[:, c, 0:nt], op0=ALU.add, op1=ALU.mult),
                        reads=[tqn, 'BB%d' % c], writes=['BB%d' % c])
                if t['last']:
                    for (sq_, c0, L, j, lxo, cuo) in segs:
                        la('pool', lambda e, sq_=sq_, lxo=lxo, L=L: e.tensor_copy(
                            out=LCO[:, :, sq_ * 3:sq_ * 3 + 3], in_=LX[:, :, lxo + L:lxo + L + 3]),
                            reads=['LX'], writes=['LCO'])
                for c in range(4):
                    la('act', lambda e, c=c: e.activation(out=BC[:, c, 0:nt], in_=BC[:, c, 0:nt], func=AF.Sqrt,
                                                          scale=-0.25, bias=epsc[:, 3:4]),
                       reads=['BC%d' % c, 'epsc'], writes=['BC%d' % c])
                    la('pool', lambda e, c=c: e.tensor_tensor(out=BA[:, c, 0:nt], in0=BA[:, c, 0:nt], in1=BC[:, c, 0:nt], op=ALU.mult),
                       reads=['BA%d' % c, 'BC%d' % c], writes=['BA%d' % c])
                for c in range(4):
                    for (sq_, c0, L, j, lxo, cuo) in segs:
                        la('dve', lambda e, c=c, c0=c0, L=L, j=j: e.tensor_tensor_scan(
                            out=BC[:, c, c0:c0 + L], data0=BD[:, c, c0:c0 + L], data1=BA[:, c, c0:c0 + L],
                            initial=hst[:, c, j:j + 1], op0=ALU.mult, op1=ALU.add),
                            reads=['BD%d' % c, 'BA%d' % c, 'hst', 'BC%d' % c], writes=['BC%d' % c])
                    la('pool', lambda e, c=c: e.tensor_tensor(out=BD[:, c, 0:nt], in0=BC[:, c, 0:nt], in1=BB[:, c, 0:nt], op=ALU.mult),
                       reads=['BC%d' % c, 'BB%d' % c, 'BD%d' % c], writes=['BD%d' % c])
                    la('act', lambda e, c=c: e.activation(out=lsq[:, 0:nt], in_=BD[:, c, 0:nt], func=AF.Square),
                       reads=['BD%d' % c], writes=['lsq'])
                    la('pe', lambda e, c=c: e.matmul(PS[6][:, 0:nt], lhsT=ones[:], rhs=lsq[:, 0:nt],
                                                     start=(c == 0), stop=(c == 3)), reads=['ones', 'lsq'], writes=['ps6'])
                allBC = ['BC%d' % c for c in range(4)]
                for (sq_, c0, L, j, lxo, cuo) in segs:
                    la('pool', lambda e, c0=c0, L=L, j=j: e.tensor_copy(out=hst[:, :, j], in_=BC[:, :, c0 + L - 1]),
                       reads=allBC + ['hst'], writes=['hst'])
                    if t['last']:
                        la('pool', lambda e, c0=c0, L=L, sq_=sq_: e.tensor_copy(out=LHO[:, :, sq_], in_=BC[:, :, c0 + L - 1]),
                           reads=allBC, writes=['LHO'])
                la('act', lambda e: e.activation(out=RS[0][:, 0:nt], in_=PS[6][:, 0:nt], func=AF.Sqrt, scale=1.0 / DL,
                                                 bias=epsc[:, 1:2]), reads=['ps6', 'epsc'], writes=['RS0'])
                la('dve', lambda e: e.reciprocal(out=RS[0][:, 0:nt], in_=RS[0][:, 0:nt]), reads=['RS0'], writes=['RS0'])
                for c in range(4):
                    la('dve', lambda e, c=c: e.scalar_tensor_tensor(
                        out=MT[:, c, 0:nt], in0=BD[:, c, 0:nt], scalar=vs(c, 42), in1=RS[0][:, 0:nt], op0=ALU.mult, op1=ALU.mult),
                        reads=['BD%d' % c, 'V5T', 'RS0'], writes=['MT'])

                for c in range(4):
                    bg, bv = 4, 5
                    zmm(ca, 12 + c, bg)
                    zmm(ca, 8 + c, bv)
                    ca('act', lambda e, bg=bg: e.activation(out=tC[:, 0:nt], in_=PS[bg][:, 0:nt], func=AF.Tanh, scale=0.5),
                       reads=['ps%d' % bg], writes=['tC'])
                    for (sq_, c0, L, j, lxo, cuo) in segs:
                        ca('dve', lambda e, c=c, bv=bv, c0=c0, L=L, cuo=cuo: e.scalar_tensor_tensor(
                            out=CU[:, c, cuo + 30:cuo + 30 + L], in0=tC[:, c0:c0 + L], scalar=1.0, in1=PS[bv][:, c0:c0 + L],
                            op0=ALU.add, op1=ALU.mult), reads=['tC', 'ps%d' % bv, 'CU'], writes=['CU'])
                        if t['last']:
                            ca('dve', lambda e, c=c, bv=bv, c0=c0, L=L, sq_=sq_: e.scalar_tensor_tensor(
                                out=CMO[:, c, sq_ * 30:sq_ * 30 + 30], in0=tC[:, c0 + L - 30:c0 + L], scalar=1.0,
                                in1=PS[bv][:, c0 + L - 30:c0 + L], op0=ALU.add, op1=ALU.mult),
                                reads=['tC', 'ps%d' % bv, 'CMO'], writes=['CMO'])
                if t['last']:
                    for (sq_, c0, L, j, lxo, cuo) in segs:
                        ca('pool', lambda e, sq_=sq_: e.tensor_scalar(
                            out=CMO[:, :, sq_ * 30:sq_ * 30 + 30], in0=CMO[:, :, sq_ * 30:sq_ * 30 + 30], scalar1=0.5,
                            scalar2=None, op0=ALU.mult), reads=['CMO'], writes=['CMO'])
                for c in range(4):
                    bank = 4 + (c % 2)
                    for (sq_, c0, L, j, lxo, cuo) in segs:
                        for k in range(31):
                            ca('pe', lambda e, c=c, k=k, bank=bank, c0=c0, L=L, cuo=cuo: e.matmul(
                                PS[bank][:, c0:c0 + L], lhsT=D31[:, c, k, :], rhs=CU[:, c, cuo + k:cuo + k + L],
                                start=(k == 0), stop=(k == 30)), reads=['D31_%d_%d' % (c, k), 'CU'], writes=['ps%d' % bank])
                    ca('act', lambda e, c=c, bank=bank: e.activation(out=BE[:, c, 0:nt], in_=PS[bank][:, 0:nt], func=AF.Identity,
                                                                     bias=vs(c, 39)), reads=['ps%d' % bank, 'V5T'], writes=['BE%d' % c])
                    ca('act', lambda e, c=c, bank=bank: e.activation(out=cb[0][:, 0:nt], in_=PS[bank][:, 0:nt], func=AF.Identity,
                                                                     bias=vs(c, 39)), reads=['ps%d' % bank, 'V5T'], writes=['cb0'])
                    ca('act', lambda e, c=c: e.activation(out=cb[1][:, 0:nt], in_=BE[:, c, 0:nt], func=AF.Square),
                       reads=['BE%d' % c], writes=['cb1'])
                    ca('pe', lambda e, c=c: e.matmul(PS[7][:, 0:nt], lhsT=ones[:], rhs=cb[0][:, 0:nt],
                                                     start=(c == 0), stop=(c == 3)), reads=['ones', 'cb0'], writes=['ps7'])
                    ca('pe', lambda e, c=c: e.matmul(PS[1][:, 0:nt], lhsT=ones[:], rhs=cb[1][:, 0:nt],
                                                     start=(c == 0), stop=(c == 3)), reads=['ones', 'cb1'], writes=['ps1'])
                ca('act', lambda e: e.activation(out=RS[1][:, 0:nt], in_=PS[7][:, 0:nt], func=AF.Copy, scale=1.0 / DL),
                   reads=['ps7'], writes=['RS1'])
                ca('dve', lambda e: e.tensor_tensor(out=RS[2][:, 0:nt], in0=RS[1][:, 0:nt], in1=RS[1][:, 0:nt], op=ALU.mult),
                   reads=['RS1'], writes=['RS2'])
                ca('dve', lambda e: e.scalar_tensor_tensor(out=RS[2][:, 0:nt], in0=PS[1][:, 0:nt], scalar=1.0 / DL, in1=RS[2][:, 0:nt],
                                                           op0=ALU.mult, op1=ALU.subtract), reads=['ps1', 'RS2'], writes=['RS2'])
                ca('act', lambda e: e.activation(out=RS[2][:, 0:nt], in_=RS[2][:, 0:nt], func=AF.Sqrt, bias=epsc[:, 0:1]),
                   reads=['RS2', 'epsc'], writes=['RS2'])
                ca('dve', lambda e: e.reciprocal(out=RS[2][:, 0:nt], in_=RS[2][:, 0:nt]), reads=['RS2'], writes=['RS2'])
                for c in range(4):
                    ca('pool', lambda e, c=c: e.tensor_tensor(out=BE[:, c, 0:nt], in0=BE[:, c, 0:nt], in1=RS[1][:, 0:nt], op=ALU.subtract),
                       reads=['BE%d' % c, 'RS1'], writes=['BE%d' % c])
                    ca('dve', lambda e, c=c: e.tensor_tensor(out=BE[:, c, 0:nt], in0=BE[:, c, 0:nt], in1=RS[2][:, 0:nt], op=ALU.mult),
                       reads=['BE%d' % c, 'RS2'], writes=['BE%d' % c])
                    ca('act', lambda e, c=c: e.activation(out=BE[:, c, 0:nt], in_=BE[:, c, 0:nt], func=AF.Identity,
                                                          scale=vs(c, 40), bias=vs(c, 41)), reads=['BE%d' % c, 'V5T'], writes=['BE%d' % c])
                    ca('act', lambda e, c=c: e.activation(out=tC[:, 0:nt], in_=BE[:, c, 0:nt], func=AF.Tanh, scale=0.5),
                       reads=['BE%d' % c], writes=['tC'])
                    ca('dve', lambda e, c=c: e.scalar_tensor_tensor(
                        out=BE[:, c, 0:nt], in0=tC[:, 0:nt], scalar=1.0, in1=BE[:, c, 0:nt], op0=ALU.add, op1=ALU.mult),
                        reads=['tC', 'BE%d' % c], writes=['BE%d' % c])
                    ca('act', lambda e, c=c: e.activation(out=cb[c % 2][:, 0:nt], in_=BE[:, c, 0:nt], func=AF.Square),
                       reads=['BE%d' % c], writes=['cb%d' % (c % 2)])
                    ca('pe', lambda e, c=c: e.matmul(PS[7][:, 0:nt], lhsT=ones[:], rhs=cb[c % 2][:, 0:nt],
                                                     start=(c == 0), stop=(c == 3)), reads=['ones', 'cb%d' % (c % 2)], writes=['ps7'])
                ca('act', lambda e: e.activation(out=RS[1][:, 0:nt], in_=PS[7][:, 0:nt], func=AF.Sqrt, scale=1.0 / DL,
                                                 bias=epsc[:, 1:2]), reads=['ps7', 'epsc', 'RS1'], writes=['RS1'])
                ca('dve', lambda e: e.reciprocal(out=RS[1][:, 0:nt], in_=RS[1][:, 0:nt]), reads=['RS1'], writes=['RS1'])
                for c in range(4):
                    ca('dve', lambda e, c=c: e.scalar_tensor_tensor(
                        out=MT[:, 4 + c, 0:nt], in0=BE[:, c, 0:nt], scalar=vs(c, 43), in1=RS[1][:, 0:nt], op0=ALU.mult, op1=ALU.mult),
                        reads=['BE%d' % c, 'V5T', 'RS1'], writes=['MT'])

                nl, ncq = len(Lq), len(Cq)
                il = ic = 0
                pref = 0
                while il < nl or ic < ncq:
                    if ic >= ncq or (il < nl and il * ncq <= ic * nl):
                        a, kw = Lq[il]
                        il += 1
                    else:
                        a, kw = Cq[ic]
                        ic += 1
                    S.add(*a, **kw)
                    done = il + ic
                    if tn is not None and pref == 0 and done >= (nl + ncq) // 3:
                        pref = 1
                        fe_norm(tn, src_p, src_s, xin, xnb, stat)

                for b in range(nb):
                    xs_ = b % 2
                    xe = xep[xs_]
                    for n in range(2):
                        yb_ = 6 + n
                        for k in range(8):
                            S.add('pe', lambda e, k=k, b=b, n=n, yb_=yb_: e.matmul(
                                PS[yb_][:, :], lhsT=MT[:, k, b * 128:(b + 1) * 128], rhs=Wout[:, k, n * 512:(n + 1) * 512],
                                start=(k == 0), stop=(k == 7)), reads=['MT', 'Wout'], writes=['ps%d' % yb_])
                        S.add('dve', lambda e, n=n, yb_=yb_, gs=gs: e.tensor_tensor(
                            out=tmp[0][:], in0=PS[yb_][:, :], in1=gate[gs][:, n * 512:(n + 1) * 512], op=ALU.mult),
                            reads=['ps%d' % yb_, 'gate%d' % gs], writes=['tmp0'])
                        S.add('pool', lambda e, n=n, xe=xe: e.tensor_tensor(
                            out=xe[:, n * 512:(n + 1) * 512], in0=xe[:, n * 512:(n + 1) * 512], in1=tmp[0][:], op=ALU.add),
                            reads=['tmp0', 'xep%d' % xs_], writes=['xep%d' % xs_])
                    S.add('sp', lambda e, b=b, xe=xe: e.dma_start(out=rows(t, b, dst_p, dst_s), in_=xe[:]),
                          reads=['xep%d' % xs_], writes=['dst2'], slot='st%d' % xs_)
                    if b + 2 < nb:
                        ld(b + 2)

            fe_norm(tiles[0], src_p, src_s, xin, xnb, stat)
            for ti, t in enumerate(tiles):
                mix_tile(ti, t)
            for (srcT, ncol, dst, nm) in ((CMO, 240, o_cm, 'CMO'), (LCO, 24, o_lc, 'LCO'), (LHO, 8, o_h, 'LHO')):
                nh = 2 if ncol == 240 else 1
                w = ncol // nh
                for hh in range(nh):
                    for c in range(4):
                        S.add('pe', lambda e, srcT=srcT, c=c, hh=hh, w=w: e.transpose(
                            out=PS[2][0:w, c * 128:(c + 1) * 128], in_=srcT[:, c, hh * w:(hh + 1) * w], identity=ident[:]),
                            reads=[nm, 'ident'], writes=['ps2'])
                    S.add('dve', lambda e, w=w: e.tensor_copy(out=ostg[0:w, :], in_=PS[2][0:w, :]), reads=['ps2', 'tA0'], writes=['tA0'])
                    S.add('sp', lambda e, dst=dst, hh=hh, w=w: e.dma_start(out=dst[hh * w:(hh + 1) * w, :], in_=ostg[0:w, :]),
                          reads=['tA0'], writes=['ostates'], slot='c1')
        S.barrier(reorder=False)

    if debug_phase == 0:
        for kk in range(9):
            S.add('sp', lambda e, kk=kk: e.dma_start(out=yp[kk * 8:(kk + 1) * 8, :], in_=modS[:, kk * D:(kk + 1) * D]),
                  slot='c1')
        S.add('sp', lambda e: e.dma_start(out=ys[:, 0:192], in_=AB_A[:].rearrange("p a b c -> p (a b c)")), reads=['AB'], slot='c1')
        S.add('sp', lambda e: e.dma_start(out=ys[:, 192:384], in_=AB_B[:].rearrange("p a b c -> p (a b c)")), reads=['AB'], slot='c1')
    elif debug_phase == 2:
        mixer_phase(xp, xs, yp, ys)
    elif debug_phase == 1:
        ffn_phase(1, xp, xs, yp, ys, 0, 2, wg1, wu1, wd1, False, None)
    else:
        ffn_phase(1, xp, xs, x1p, x1s, 0, 2, wg1, wu1, wd1, False, None)
        mixer_phase(x1p, x1s, x2p, x2s)
        ffn_phase(3, x2p, x2s, yp, ys, 2, 8, wg2, wu2, wd2, True, None)

    S.barrier()
    S.add('sp', lambda e: e.nop())
    S.emit(nc, es)
    es.close()
    return nc


def _prep_inputs(inp):
    f = lambda a: np.ascontiguousarray(np.asarray(a, dtype=np.float32))
    v5 = np.concatenate([
        f(inp['lru_conv_w'])[0], f(inp['lru_conv_b']), f(inp['lru_ba']), f(inp['lru_bx']), f(inp['lru_lambda']),
        f(inp['cm_dw_w'])[0], f(inp['cm_dw_b']), f(inp['cm_ln_g']), f(inp['cm_ln_b']),
        f(inp['out_g_lru']), f(inp['out_g_conv'])], axis=0)
    assert v5.shape == (44, DL)
    shared = dict(
        w_ada=f(inp['w_ada'])[0], b_ada=f(inp['b_ada']),
        g3=np.concatenate([f(inp['ffn1_g']), f(inp['mix_g']), f(inp['ffn2_g']), f(inp['final_g'])[None, :]], axis=0),
        fgain=f(inp['final_g'])[None, :],
        wg1=f(inp['ffn1_wg'])[0], wu1=f(inp['ffn1_wu'])[0], wd1=f(inp['ffn1_wd'])[0],
        wg2=f(inp['ffn2_wg'])[0], wu2=f(inp['ffn2_wu'])[0], wd2=f(inp['ffn2_wd'])[0],
        w_in=f(inp['w_in'])[0], w_out=f(inp['w_out'])[0], v5=v5,
        wa=f(inp['lru_wa'])[0].reshape(512, 64), wx=f(inp['lru_wx'])[0].reshape(512, 64),
        identd=np.eye(128, dtype=np.float32),
    )
    xp = f(inp['x_prompt']); xs = f(inp['x_sample'])
    sh = f(inp['state_lru_h'])[0]; slc = f(inp['state_lru_conv'])[0]; scm = f(inp['state_cm_conv'])[0]
    cp = f(inp['c_prompt']); cs = f(inp['c_sample'])
    maps = []
    for c in range(NCORES):
        m = dict(shared)
        m['xp'] = xp[4 * c:4 * c + 4].reshape(NPT, D)
        m['xs'] = xs[4 * c:4 * c + 4].reshape(NST, D)
        m['st_h'] = sh[4 * c:4 * c + 4]
        m['st_lc'] = slc[4 * c:4 * c + 4].reshape(12, DL)
        m['st_cm'] = scm[4 * c:4 * c + 4].reshape(120, DL)
        m['cvec'] = np.concatenate([cp[4 * c:4 * c + 4], cs[4 * c:4 * c + 4]], axis=0)
        maps.append(m)
    return maps


def kernel(**inputs):
    maps = _prep_inputs(inputs)
    nc = build_program()
    res = run_bass_kernel_spmd(nc, maps, core_ids=list(range(NCORES)))
    R = res.results
    y_p = np.concatenate([r['yp'].reshape(4, SEQ, D) for r in R], axis=0)
    y_s = np.concatenate([r['ys'].reshape(4, DSEQ, D) for r in R], axis=0)
    hp = np.concatenate([r['o_h'][0:4] for r in R], axis=0)[None]
    hs = np.concatenate([r['o_h'][4:8] for r in R], axis=0)[None]
    lcp = np.concatenate([r['o_lc'][0:12].reshape(4, 3, DL) for r in R], axis=0)[None]
    lcs = np.concatenate([r['o_lc'][12:24].reshape(4, 3, DL) for r in R], axis=0)[None]
    cmp_ = np.concatenate([r['o_cm'][0:120].reshape(4, 30, DL) for r in R], axis=0)[None]
    cms = np.concatenate([r['o_cm'][120:240].reshape(4, 30, DL) for r in R], axis=0)[None]
    return (y_p, y_s, hp, lcp, cmp_, hs, lcs, cms)
```
